# Optimizing a Trainium2 kernel written in Bass

```python
import jax, jax.numpy as jnp
from jax import lax
import numpy as np

D_MODEL = 1024
BATCH = 16
SEQ = 4096
DEPTH = 2

GRID_W = 64
CTX_LEN = 256
HEAD_DIM = 64
ATTN_WIDTH = D_MODEL // 2
N_Q_HEADS = ATTN_WIDTH // HEAD_DIM
N_KV_HEADS = N_Q_HEADS // 4
GQA_GROUP = N_Q_HEADS // N_KV_HEADS
KV_WIDTH = N_KV_HEADS * HEAD_DIM
POOL_WINDOWS = (2, 4, 8, 16)
N_POOL_GROUPS = len(POOL_WINDOWS)
POOL_WIDTH = D_MODEL // 4
POOL_GROUP = POOL_WIDTH // N_POOL_GROUPS
N_FOURIER_GROUPS = 4
FOURIER_WIDTH = D_MODEL // 4
FOURIER_GROUP = FOURIER_WIDTH // N_FOURIER_GROUPS
MIX_WIDTH = ATTN_WIDTH + POOL_WIDTH + FOURIER_WIDTH
Q_END = ATTN_WIDTH
K_END = Q_END + KV_WIDTH
KV_END = K_END + KV_WIDTH
POOL_END = KV_END + POOL_WIDTH
IN_WIDTH = POOL_END + FOURIER_WIDTH
D_FF = -(-8 * D_MODEL // (3 * 256)) * 256
Q_BLOCK = 128
ROPE_THETA = 10000.0
EPS = 1e-6

kernel_name = "hybrid_attn_pool_fourier_dit_block"


def rms_norm(x, g):
    xf = x.astype(jnp.float32)
    y = xf * lax.rsqrt(jnp.mean(xf * xf, axis=-1, keepdims=True) + EPS)
    return (y * g.astype(jnp.float32)).astype(x.dtype)


def axial_rope_tables(length):
    rows = length // GRID_W
    row_ids = jnp.repeat(jnp.arange(rows), GRID_W).astype(jnp.float32)
    col_ids = jnp.tile(jnp.arange(GRID_W), rows).astype(jnp.float32)
    n_freq = HEAD_DIM // 4
    freqs = ROPE_THETA ** (-jnp.arange(n_freq, dtype=jnp.float32) / n_freq)
    ang = jnp.stack([row_ids[:, None] * freqs, col_ids[:, None] * freqs], axis=1)
    return jnp.cos(ang), jnp.sin(ang)


def apply_rope(x, cos, sin):
    n_freq = HEAD_DIM // 4
    xf = x.astype(jnp.float32).reshape(*x.shape[:-1], 2, 2, n_freq)
    x1, x2 = xf[..., 0, :], xf[..., 1, :]
    cb, sb = cos[None, :, None], sin[None, :, None]
    out = jnp.stack([x1 * cb - x2 * sb, x2 * cb + x1 * sb], axis=-2)
    return out.reshape(x.shape).astype(x.dtype)


def gqa_softmax(qg, k, v):
    s = jnp.einsum('bqgrd,bkgd->bgrqk', qg, k, preferred_element_type=jnp.float32) * (HEAD_DIM ** -0.5)
    p = jax.nn.softmax(s, axis=-1).astype(v.dtype)
    return jnp.einsum('bgrqk,bkgd->bqgrd', p, v)


def latent_attention(q, k, v, k_ctx, v_ctx):
    b, s = q.shape[:2]
    n_blk = s // Q_BLOCK
    k_all = jnp.concatenate([k_ctx, k], axis=1)
    v_all = jnp.concatenate([v_ctx, v], axis=1)
    qb = q.reshape(b, n_blk, Q_BLOCK, N_KV_HEADS, GQA_GROUP, HEAD_DIM).transpose(1, 0, 2, 3, 4, 5)
    o = lax.map(lambda qblk: gqa_softmax(qblk, k_all, v_all), qb)
    return o.transpose(1, 0, 2, 3, 4, 5).reshape(b, s, ATTN_WIDTH)


def context_attention(q, k, v):
    b, cl = q.shape[:2]
    qg = q.reshape(b, cl, N_KV_HEADS, GQA_GROUP, HEAD_DIM)
    return gqa_softmax(qg, k, v).reshape(b, cl, ATTN_WIDTH)


def pool_mix(u, w_pool, pool_scale):
    b, length, _ = u.shape
    ug = u.astype(jnp.float32).reshape(b, length, N_POOL_GROUPS, POOL_GROUP)
    cs = jnp.concatenate([jnp.zeros((b, 1, N_POOL_GROUPS, POOL_GROUP), jnp.float32),
                          jnp.cumsum(ug, axis=1)], axis=1)
    t = jnp.arange(length)
    means = []
    for gi, w in enumerate(POOL_WINDOWS):
        lo = jnp.clip(t - w // 2, 0, length)
        hi = jnp.clip(t + w - w // 2, 0, length)
        cs_g = cs[:, :, gi]
        cnt = (hi - lo).astype(jnp.float32)
        means.append((cs_g[:, hi] - cs_g[:, lo]) / cnt[None, :, None])
    d = jnp.stack(means, axis=2) - ug
    y = jnp.einsum('blgc,gcd->blgd', d, w_pool.astype(jnp.float32)).reshape(b, length, POOL_WIDTH)
    return (y * pool_scale.astype(jnp.float32)).astype(u.dtype)


def fourier_mix(u, w_four):
    b, length, _ = u.shape
    ug = u.astype(jnp.float32).reshape(b, length, N_FOURIER_GROUPS, FOURIER_GROUP)
    f = jnp.fft.fft2(ug, axes=(1, 3), norm='ortho').real
    y = jnp.einsum('blgc,gcd->blgd', f, w_four.astype(jnp.float32))
    return y.reshape(b, length, FOURIER_WIDTH).astype(u.dtype)


def head_rms(x, g):
    return rms_norm(x, g)


def swiglu(h, w_gate_up, w_down):
    gate, up = jnp.split(h @ w_gate_up, 2, axis=-1)
    return (jax.nn.silu(gate) * up) @ w_down


def setup_inputs(seed: int = 0) -> dict:
    key = jax.random.key(seed)
    ks = jax.random.split(key, 18)
    f32 = jnp.float32

    def nrm(k, shape, s):
        return jax.random.normal(k, shape, f32) * s

    return {
        "x": nrm(ks[0], (BATCH, SEQ, D_MODEL), 1.0),
        "c": nrm(ks[1], (BATCH, D_MODEL), 1.0),
        "ctx": nrm(ks[2], (BATCH, CTX_LEN, D_MODEL), 1.0),
        "c_ctx": nrm(ks[3], (D_MODEL,), 1.0),
        "w_ada": nrm(ks[4], (DEPTH, D_MODEL, 6 * D_MODEL), D_MODEL ** -0.5),
        "b_ada": nrm(ks[5], (DEPTH, 6 * D_MODEL), 0.01),
        "g_mix": 1.0 + nrm(ks[6], (DEPTH, D_MODEL), 0.02),
        "g_ffn": 1.0 + nrm(ks[7], (DEPTH, D_MODEL), 0.02),
        "w_in": nrm(ks[8], (DEPTH, D_MODEL, IN_WIDTH), D_MODEL ** -0.5),
        "q_gain": 1.0 + nrm(ks[9], (DEPTH, HEAD_DIM), 0.02),
        "k_gain": 1.0 + nrm(ks[10], (DEPTH, HEAD_DIM), 0.02),
        "w_pool": nrm(ks[11], (DEPTH, N_POOL_GROUPS, POOL_GROUP, POOL_GROUP), POOL_GROUP ** -0.5),
        "pool_scale": 1.0 + nrm(ks[12], (DEPTH, POOL_WIDTH), 0.02),
        "w_four": nrm(ks[13], (DEPTH, N_FOURIER_GROUPS, FOURIER_GROUP, FOURIER_GROUP), FOURIER_GROUP ** -0.5),
        "w_out": nrm(ks[14], (DEPTH, MIX_WIDTH, D_MODEL), MIX_WIDTH ** -0.5),
        "w_gate_up": nrm(ks[15], (DEPTH, D_MODEL, 2 * D_FF), D_MODEL ** -0.5),
        "w_down": nrm(ks[16], (DEPTH, D_FF, D_MODEL), D_FF ** -0.5),
    }


def reference(x, c, ctx, c_ctx, w_ada, b_ada, g_mix, g_ffn, w_in, q_gain, k_gain,
              w_pool, pool_scale, w_four, w_out, w_gate_up, w_down):
    b, s, _ = x.shape
    cl = ctx.shape[1]
    cos, sin = axial_rope_tables(s)
    sc, scc = jax.nn.silu(c), jax.nn.silu(c_ctx)
    for i in range(DEPTH):
        update_ctx = i < DEPTH - 1
        mod_x = (sc @ w_ada[i] + b_ada[i])[:, None, :]
        mod_c = scc @ w_ada[i] + b_ada[i]
        sh1, sc1, ga1, sh2, sc2, ga2 = jnp.split(mod_x, 6, axis=-1)
        csh1, csc1, cga1, csh2, csc2, cga2 = jnp.split(mod_c, 6, axis=-1)

        hx = rms_norm(x, g_mix[i]) * (1.0 + sc1) + sh1
        hc = rms_norm(ctx, g_mix[i]) * (1.0 + csc1) + csh1
        zx = hx @ w_in[i]
        if update_ctx:
            zc = hc @ w_in[i]
            zc_kv = zc[..., Q_END:KV_END]
        else:
            zc_kv = hc @ w_in[i][:, Q_END:KV_END]
        kc = head_rms(zc_kv[..., :KV_WIDTH].reshape(b, cl, N_KV_HEADS, HEAD_DIM), k_gain[i])
        vc = zc_kv[..., KV_WIDTH:].reshape(b, cl, N_KV_HEADS, HEAD_DIM)

        q = apply_rope(head_rms(zx[..., :Q_END].reshape(b, s, N_Q_HEADS, HEAD_DIM), q_gain[i]), cos, sin)
        k = apply_rope(head_rms(zx[..., Q_END:K_END].reshape(b, s, N_KV_HEADS, HEAD_DIM), k_gain[i]), cos, sin)
        v = zx[..., K_END:KV_END].reshape(b, s, N_KV_HEADS, HEAD_DIM)
        mx = jnp.concatenate([
            latent_attention(q, k, v, kc, vc),
            pool_mix(zx[..., KV_END:POOL_END], w_pool[i], pool_scale[i]),
            fourier_mix(zx[..., POOL_END:], w_four[i]),
        ], axis=-1)
        x = x + ga1 * (mx @ w_out[i])

        if update_ctx:
            qc = head_rms(zc[..., :Q_END].reshape(b, cl, N_Q_HEADS, HEAD_DIM), q_gain[i])
            mc = jnp.concatenate([
                context_attention(qc, kc, vc),
                pool_mix(zc[..., KV_END:POOL_END], w_pool[i], pool_scale[i]),
                fourier_mix(zc[..., POOL_END:], w_four[i]),
            ], axis=-1)
            ctx = ctx + cga1 * (mc @ w_out[i])

        fx = rms_norm(x, g_ffn[i]) * (1.0 + sc2) + sh2
        x = x + ga2 * swiglu(fx, w_gate_up[i], w_down[i])
        if update_ctx:
            fc = rms_norm(ctx, g_ffn[i]) * (1.0 + csc2) + csh2
            ctx = ctx + cga2 * swiglu(fc, w_gate_up[i], w_down[i])
    return x
```

```python
import numpy as np
import ml_dtypes
from contextlib import ExitStack
import concourse.bass as bass
import concourse.mybir as mybir
from concourse.bass_utils import run_bass_kernel_spmd

F32 = mybir.dt.float32
BF16 = mybir.dt.bfloat16
AF = mybir.ActivationFunctionType
ALU = mybir.AluOpType
NPBF = ml_dtypes.bfloat16

NCORES = 8
NB = 2
T = 4096
CTX = 256
TT = T + CTX
D = 1024
DFF = 2816
EPS = 1e-6
POOLW = (2, 4, 8, 16)
TILES = [(0, 256, True)] + [(256 + 512 * i, 512, False) for i in range(8)]
FT = 256
FTILES = [(0, 256, True)] + [(256 + FT * i, FT, False) for i in range(T // FT)]
NST = TT // 128


class Buf:
    __slots__ = ("w", "r")

    def __init__(self):
        self.w = None
        self.r = {}


class TB:
    def __init__(self, t, n=1):
        self.t = t
        self.b = [Buf() for _ in range(n)]


class Rot:
    def __init__(self, items):
        self.items = items
        self.i = 0

    def next(self):
        it = self.items[self.i]
        self.i = (self.i + 1) % len(self.items)
        return it


NDS = 8


class KB:
    def __init__(self, nc, es):
        self.nc = nc
        self.E = dict(pe=nc.tensor, act=nc.scalar, dve=nc.vector, pool=nc.gpsimd, sp=nc.sync)
        self.csem = {e: es.enter_context(nc.semaphore("c_" + e)) for e in ["pe", "act", "dve", "pool"]}
        self.cnt = {e: 0 for e in self.csem}
        self.waited = {e: {} for e in self.E}
        self.dsem = {}
        for q in ["sp", "pool"]:
            self.dsem[q] = [[es.enter_context(nc.semaphore(f"d_{q}{i}")), 0] for i in range(NDS)]
        self.drr = {q: 0 for q in self.dsem}
        self.nwait = 0

    def _wait(self, eng, h):
        key, sem, val = h
        if eng == "pe" and key == "c_pe":
            return
        w = self.waited[eng]
        if w.get(key, 0) >= val:
            return
        self.E[eng].wait_ge(sem, val)
        self.nwait += 1
        w[key] = val

    def _deps(self, eng, reads, writes):
        for b in reads:
            if b.w is not None:
                self._wait(eng, b.w)
        for b in writes:
            if b.w is not None:
                self._wait(eng, b.w)
            for h in b.r.values():
                self._wait(eng, h)

    @staticmethod
    def _mark(h, reads, writes):
        for b in reads:
            old = b.r.get(h[0])
            if old is None or old[2] < h[2]:
                b.r[h[0]] = h
        for b in writes:
            b.w = h
            b.r = {}

    def op(self, eng, ins_fn, reads=(), writes=()):
        self._deps(eng, reads, writes)
        ins = ins_fn()
        self.cnt[eng] += 1
        ins.then_inc(self.csem[eng], 1)
        h = ("c_" + eng, self.csem[eng], self.cnt[eng])
        self._mark(h, reads, writes)
        return h

    def dma(self, q, out, in_, reads=(), writes=()):
        self._deps(q, reads, writes)
        idx = self.drr[q]
        self.drr[q] = (idx + 1) % NDS
        slot = self.dsem[q][idx]
        sem, c = slot
        key = f"d_{q}{idx}"
        if c > 0:
            self._wait(q, (key, sem, 16 * c))
        ins = self.E[q].dma_start(out=out, in_=in_)
        ins.then_inc(sem, 16)
        slot[1] = c + 1
        h = (key, sem, 16 * (c + 1))
        self._mark(h, reads, writes)
        return h

    def barrier(self):
        hs = [("c_" + e, self.csem[e], self.cnt[e]) for e in self.csem if self.cnt[e] > 0]
        for q, slots in self.dsem.items():
            for i, (sem, c) in enumerate(slots):
                if c > 0:
                    hs.append((f"d_{q}{i}", sem, 16 * c))
        for eng in self.E:
            for h in hs:
                key, sem, val = h
                w = self.waited[eng]
                if w.get(key, 0) >= val:
                    continue
                self.E[eng].wait_ge(sem, val)
                w[key] = val


def build_program(debug=False, stop_after=None):
    nc = bass.Bass("TRN2", target_bir_lowering=False)
    d = {}

    def din(name, shape, dt):
        d[name] = nc.dram_tensor(name, shape, dt, kind="ExternalInput").ap()

    din("xT", [NB, 8, 128, T], F32)
    din("ctxT", [NB, 8, 128, CTX], F32)
    din("cc", [128, 8, 4], F32)
    din("w_ada", [2, 8, 128, 6144], F32)
    din("b_ada", [2, 128, 48], F32)
    din("g_mix", [2, 128, 8], F32)
    din("g_ffn", [2, 128, 8], F32)
    din("w_in", [2, 8, 128, 1280], F32)
    din("qk_gain", [2, 128, 2], F32)
    din("w_pool", [2, 128, 2, 128], F32)
    din("pool_scale", [2, 128, 2], F32)
    din("w_four", [2, 128, 2, 128], F32)
    din("w_out", [2, 8, 128, 1024], F32)
    din("w_gu", [2, 8, 128, 2 * DFF], F32)
    din("w_dn", [2, 22, 128, 1024], F32)
    din("rope_cos", [128, T], F32)
    din("rope_sin", [128, T], F32)
    din("prot", [128, 128], BF16)
    din("ccbd", [128, 128], F32)
    din("scbd", [128, 128], F32)
    din("dft_c", [32, 128, T], BF16)
    din("dft_s", [32, 128, T], BF16)
    din("dft256", [128, 2, 2, 256], BF16)
    din("bands", [128, 4 * 5 * 128], BF16)
    skind = "ExternalOutput" if debug else "Internal"
    yT = nc.dram_tensor("yT", [NB, 8, 128, T], F32, kind="ExternalOutput").ap()
    x1s = nc.dram_tensor("x1s", [NB, 8, 128, TT], F32, kind=skind).ap()
    x2s = nc.dram_tensor("x2s", [NB, 8, 128, TT], F32, kind=skind).ap()
    rs2 = nc.dram_tensor("rs2", [NB, 128, TT], F32, kind=skind).ap()
    dbg = {}
    if debug:
        dbg["qt"] = nc.dram_tensor("dbg_qt", [128, 4, TT], BF16, kind="ExternalOutput").ap()
        dbg["kt"] = nc.dram_tensor("dbg_kt", [128, TT], BF16, kind="ExternalOutput").ap()
        dbg["vp"] = nc.dram_tensor("dbg_vp", [128, NST, 448], BF16, kind="ExternalOutput").ap()
        dbg["uab"] = nc.dram_tensor("dbg_uab", [128, NST, 512], BF16, kind="ExternalOutput").ap()
        dbg["mod"] = nc.dram_tensor("dbg_mod", [128, 2, 48, 4], F32, kind="ExternalOutput").ap()
        dbg["mx"] = nc.dram_tensor("dbg_mx", [128, 8, TT], BF16, kind="ExternalOutput").ap()

    x1b = [[Buf() for _ in TILES] for _ in range(NB)]
    x2b = [[Buf() for _ in TILES] for _ in range(NB)]
    rsb = [[Buf() for _ in TILES] for _ in range(NB)]
    CONSTB = Buf()

    with ExitStack() as es:
        k = KB(nc, es)

        uid = [0]

        def sb(name, shape, dt, n=1, stack=None):
            uid[0] += 1
            t = (stack or es).enter_context(nc.sbuf_tensor(f"s{uid[0]}_{name}", shape, dt))
            return TB(t, n)

        psum_all = es.enter_context(nc.psum_tensor("psum_all", [128, 4096], F32))
        PSB = [Buf() for _ in range(8)]

        class PS:
            def __init__(self, i):
                self.i = i
                self.t = psum_all[:, i * 512:(i + 1) * 512]
                self.b = PSB[i]

        PSUM = Rot([PS(i) for i in range(8)])

        ones = sb("ones", [128, 128], BF16)
        bones = sb("bones", [128, 128], BF16)
        prot = sb("prot", [128, 128], BF16)
        bands = sb("bands", [128, 2560], BF16)
        d256 = sb("d256", [128, 2, 2, 256], BF16)
        ccbd = sb("ccbd", [128, 128], F32)
        scbd = sb("scbd", [128, 128], F32)
        scT = sb("scT", [128, 8, 4], F32)
        modT = sb("modT", [128, 2, 48, 4], F32)
        bada = sb("bada", [128, 2, 48], F32)
        gmix = sb("gmix", [128, 2, 8], F32)
        gffn = sb("gffn", [128, 2, 8], F32)
        qkg = sb("qkg", [128, 2, 2], F32)
        pscale = sb("pscale", [128, 2, 2], F32)
        wpb = sb("wpb", [128, 2, 2, 128], BF16)
        wf32 = sb("wf32", [128, 2, 2, 128], F32)
        abd = sb("abd", [128, 2, 4, 128], BF16)
        gsc1 = sb("gsc1", [128, 2, 8, 4], F32)
        gsc2 = sb("gsc2", [128, 2, 8, 4], F32)
        tmpm = sb("tmpm", [128, 8, 4], F32)

        V, Pq, A, S_ = nc.vector, nc.gpsimd, nc.scalar, nc.sync
        k.op("dve", lambda: V.memset(ones.t[:], 1.0), writes=ones.b)
        k.op("dve", lambda: V.memset(bones.t[:], 0.0), writes=bones.b)
        k.op("dve", lambda: V.memset(bones.t[0:64, 0:64], 1.0), writes=bones.b)
        k.op("dve", lambda: V.memset(bones.t[64:128, 64:128], 1.0), writes=bones.b)
        k.dma("sp", prot.t[:], d["prot"][:, :], writes=prot.b)
        k.dma("sp", bands.t[:], d["bands"][:, :], writes=bands.b)
        k.dma("sp", d256.t[:], d["dft256"][:, :, :, :], writes=d256.b)
        k.dma("sp", ccbd.t[:], d["ccbd"][:, :], writes=ccbd.b)
        k.dma("sp", scbd.t[:], d["scbd"][:, :], writes=scbd.b)
        k.dma("sp", scT.t[:], d["cc"][:, :, :], writes=scT.b)
        for l in range(2):
            k.dma("sp", bada.t[:, l, :], d["b_ada"][l], writes=bada.b)
            k.dma("sp", gmix.t[:, l, :], d["g_mix"][l], writes=gmix.b)
            k.dma("sp", gffn.t[:, l, :], d["g_ffn"][l], writes=gffn.b)
            k.dma("sp", qkg.t[:, l, :], d["qk_gain"][l], writes=qkg.b)
            k.dma("sp", pscale.t[:, l, :], d["pool_scale"][l], writes=pscale.b)
            k.dma("sp", wf32.t[:, l, :, :], d["w_four"][l], writes=wf32.b)
            k.dma("pool", wpb.t[:, l, :, :], d["w_pool"][l], writes=wpb.b)
        k.op("act", lambda: A.activation(out=scT.t[:], in_=scT.t[:], func=AF.Silu), reads=scT.b, writes=scT.b)
        for l in range(2):
            for ab in range(2):
                for fc in range(2):
                    ps = PSUM.next()
                    cs = ccbd if ab == 0 else scbd
                    k.op("pe", lambda: nc.tensor.matmul(ps.t[:, 0:128], lhsT=cs.t[:], rhs=wf32.t[:, l, fc, :], start=True, stop=True),
                         reads=cs.b + wf32.b, writes=[ps.b])
                    k.op("dve", lambda: V.tensor_copy(out=abd.t[:, l, ab * 2 + fc, :], in_=ps.t[:, 0:128]), reads=[ps.b], writes=abd.b)

        with ExitStack() as ph:
            wa = Rot([sb(f"wa{i}", [128, 8, 512], F32, stack=ph) for i in range(2)])
            for l in range(2):
                for jb in range(12):
                    w = wa.next()
                    k.dma("sp", w.t[:], d["w_ada"][l].rearrange("c p n -> p c n")[:, :, jb * 512:(jb + 1) * 512], writes=w.b)
                    for jj in range(4):
                        j = jb * 4 + jj
                        ps = PSUM.next()
                        for dc in range(8):
                            k.op("pe", lambda: nc.tensor.matmul(ps.t[:, 0:4], lhsT=w.t[:, dc, jj * 128:(jj + 1) * 128], rhs=scT.t[:, dc, :],
                                                                start=(dc == 0), stop=(dc == 7)), reads=w.b + scT.b, writes=[ps.b])
                        k.op("dve", lambda: V.tensor_scalar(out=modT.t[:, l, j, :], in0=ps.t[:, 0:4], scalar1=bada.t[:, l, j:j + 1], scalar2=None,
                                                            op0=ALU.add), reads=[ps.b] + bada.b, writes=modT.b)
                for (gs, g, base) in ((gsc1, gmix, 8), (gsc2, gffn, 32)):
                    k.op("dve", lambda: V.tensor_scalar(out=tmpm.t[:], in0=modT.t[:, l, base:base + 8, :], scalar1=1.0, scalar2=None, op0=ALU.add),
                         reads=modT.b, writes=tmpm.b)
                    for col in range(3):
                        k.op("dve", lambda: V.tensor_tensor(out=gs.t[:, l, :, col], in0=tmpm.t[:, :, col], in1=g.t[:, l, :], op=ALU.mult),
                             reads=tmpm.b + g.b, writes=gs.b)
            if debug:
                k.dma("sp", dbg["mod"][:, :, :, :], modT.t[:], reads=modT.b)
            k.barrier()

        def xsrc(l, b, ti):
            T0, N, isctx = TILES[ti]
            if l == 0:
                if isctx:
                    return d["ctxT"][b].rearrange("c p n -> p c n"), None
                return d["xT"][b].rearrange("c p n -> p c n")[:, :, T0 - CTX:T0 - CTX + N], None
            return x2s[b].rearrange("c p n -> p c n")[:, :, T0:T0 + N], x2b[b][ti]

        def load_w(wt, src, ncol, rows):
            for c0 in range(0, ncol, 1024):
                c1 = min(ncol, c0 + 1024)
                for r0 in range(0, rows, 8):
                    r1 = min(rows, r0 + 8)
                    k.dma("pool", wt.t[:, r0:r1, c0:c1], src.rearrange("c p n -> p c n")[:, r0:r1, c0:c1], writes=wt.b)

        for l in range(2):
            for b in range(NB):
                with ExitStack() as mix:
                    QT = sb("QT", [128, 4, TT], BF16, stack=mix)
                    QTb = [[Buf() for _ in TILES] for _ in range(4)]
                    KT = sb("KT", [128, TT], BF16, n=len(TILES), stack=mix)
                    VP = sb("VP", [128, NST, 448], BF16, n=len(TILES), stack=mix)
                    k.op("dve", lambda: V.memset(VP.t[:, :, 64:128], 1.0), writes=VP.b)
                    UAB = sb("UAB", [128, NST, 512], BF16, n=len(TILES), stack=mix)
                    XT = sb("XT", [128, 8, 512], F32, n=8, stack=mix)
                    sq = sb("sq", [128, 8, 512], BF16, n=8, stack=mix)
                    lnv = sb("lnv", [128, 512], F32, stack=mix)
                    rstd = sb("rstd", [128, 512], F32, stack=mix)
                    with ExitStack() as ph:
                        win = sb("win", [128, 8, 1280], BF16, stack=ph)
                        load_w(win, d["w_in"][l], 1280, 8)
                        hTs = [sb(f"hT{i}", [128, 8, 512], BF16, n=8, stack=ph) for i in range(2)]
                        cs = sb("cs", [128, 2, 512], F32, stack=ph)
                        zsqR = Rot([sb(f"zsq{i}", [128, 512], BF16, stack=ph) for i in range(2)])
                        hrR = Rot([sb(f"hr{i}", [128, 512], F32, stack=ph) for i in range(2)])
                        zgR = Rot([sb(f"zg{i}", [128, 512], BF16, stack=ph) for i in range(2)])
                        t1R = Rot([sb(f"t1{i}", [128, 512], F32, stack=ph) for i in range(2)])
                        t2R = Rot([sb(f"t2{i}", [128, 512], F32, stack=ph) for i in range(1)])
                        uT = [sb(f"uT{i}", [128, 512], BF16, stack=ph) for i in range(2)]

                        def normA(ti):
                            T0, N, isctx = TILES[ti]
                            col = 2 if isctx else b
                            hT = hTs[ti % 2]
                            src, srcb = xsrc(l, b, ti)
                            k.dma("sp", XT.t[:, :, 0:N], src, reads=[srcb] if srcb else [], writes=XT.b)
                            k.op("act", lambda: A.activation(out=sq.t[:, :, 0:N], in_=XT.t[:, :, 0:N], func=AF.Square), reads=XT.b, writes=sq.b)
                            ps = PSUM.next()
                            for c in range(8):
                                k.op("pe", lambda: nc.tensor.matmul(ps.t[:, 0:N], lhsT=ones.t[:], rhs=sq.t[:, c, 0:N], start=(c == 0), stop=(c == 7)),
                                     reads=sq.b, writes=[ps.b])
                            k.op("act", lambda: A.activation(out=lnv.t[:, 0:N], in_=ps.t[:, 0:N], func=AF.Ln, scale=1.0 / D, bias=EPS), reads=[ps.b], writes=lnv.b)
                            k.op("act", lambda: A.activation(out=rstd.t[:, 0:N], in_=lnv.t[:, 0:N], func=AF.Exp, scale=-0.5), reads=lnv.b, writes=rstd.b)
                            for c in range(8):
                                if c % 2 == 0:
                                    k.op("pool", lambda: Pq.tensor_tensor(out=XT.t[:, c, 0:N], in0=XT.t[:, c, 0:N], in1=rstd.t[:, 0:N], op=ALU.mult),
                                         reads=[XT.b[c]] + rstd.b, writes=[XT.b[c]])
                                else:
                                    k.op("dve", lambda: V.tensor_tensor(out=XT.t[:, c, 0:N], in0=XT.t[:, c, 0:N], in1=rstd.t[:, 0:N], op=ALU.mult),
                                         reads=[XT.b[c]] + rstd.b, writes=[XT.b[c]])
                                k.op("act", lambda: A.activation(out=hT.t[:, c, 0:N], in_=XT.t[:, c, 0:N], func=AF.Identity, scale=gsc1.t[:, l, c, col:col + 1],
                                                                 bias=modT.t[:, l, c, col:col + 1]), reads=[XT.b[c]], writes=[hT.b[c]])

                        normA(0)
                        for ti, (T0, N, isctx) in enumerate(TILES):
                            col = 2 if isctx else b
                            full = not (isctx and l == 1)
                            st0 = T0 // 128
                            hT = hTs[ti % 2]
                            if not isctx:
                                k.dma("sp", cs.t[:, 0, 0:N], d["rope_cos"][:, T0 - CTX:T0 - CTX + N], writes=cs.b)
                                k.dma("sp", cs.t[:, 1, 0:N], d["rope_sin"][:, T0 - CTX:T0 - CTX + N], writes=cs.b)
                            ocs = list(range(7)) if full else [4]
                            stt = {}

                            def stage0(oc):
                                ps = PSUM.next()
                                for dc in range(8):
                                    k.op("pe", lambda: nc.tensor.matmul(ps.t[:, 0:N], lhsT=win.t[:, dc, oc * 128:(oc + 1) * 128], rhs=hT.t[:, dc, 0:N],
                                                                        start=(dc == 0), stop=(dc == 7)), reads=[hT.b[dc]] + win.b, writes=[ps.b])
                                stt[oc] = {"ps": ps}

                            def stage1(oc):
                                ps = stt[oc]["ps"]
                                if oc >= 5:
                                    u = uT[oc - 5]
                                    k.op("dve", lambda: V.tensor_copy(out=u.t[:, 0:N], in_=ps.t[:, 0:N]), reads=[ps.b], writes=u.b)
                                    return
                                zsq = zsqR.next()
                                k.op("act", lambda: A.activation(out=zsq.t[:, 0:N], in_=ps.t[:, 0:N], func=AF.Square), reads=[ps.b], writes=zsq.b)
                                hs = PSUM.next()
                                k.op("pe", lambda: nc.tensor.matmul(hs.t[:, 0:N], lhsT=bones.t[:], rhs=zsq.t[:, 0:N], start=True, stop=True),
                                     reads=zsq.b, writes=[hs.b])
                                hr = hrR.next()
                                k.op("act", lambda: A.activation(out=hr.t[:, 0:N], in_=hs.t[:, 0:N], func=AF.Ln, scale=1.0 / 64, bias=EPS), reads=[hs.b], writes=hr.b)
                                k.op("act", lambda: A.activation(out=hr.t[:, 0:N], in_=hr.t[:, 0:N], func=AF.Exp, scale=-0.5), reads=hr.b, writes=hr.b)
                                gcol = 0 if oc < 4 else 1
                                if oc < 4:
                                    dest, destb = QT.t[:, oc, T0:T0 + N], [QTb[oc][ti]]
                                else:
                                    dest, destb = KT.t[:, T0:T0 + N], [KT.b[ti]]
                                stt[oc]["dest"] = (dest, destb)
                                if isctx:
                                    k.op("dve", lambda: V.scalar_tensor_tensor(out=dest, in0=ps.t[:, 0:N], scalar=qkg.t[:, l, gcol:gcol + 1], in1=hr.t[:, 0:N],
                                                                               op0=ALU.mult, op1=ALU.mult), reads=[ps.b] + hr.b, writes=destb)
                                else:
                                    zg = zgR.next()
                                    stt[oc]["zg"] = zg
                                    k.op("dve", lambda: V.scalar_tensor_tensor(out=zg.t[:, 0:N], in0=ps.t[:, 0:N], scalar=qkg.t[:, l, gcol:gcol + 1], in1=hr.t[:, 0:N],
                                                                               op0=ALU.mult, op1=ALU.mult), reads=[ps.b] + hr.b, writes=zg.b)

                            def stage2(oc):
                                if oc >= 5 or isctx:
                                    return
                                zg = stt[oc]["zg"]
                                dest, destb = stt[oc]["dest"]
                                zr = PSUM.next()
                                k.op("pe", lambda: nc.tensor.matmul(zr.t[:, 0:N], lhsT=prot.t[:], rhs=zg.t[:, 0:N], start=True, stop=True), reads=zg.b, writes=[zr.b])
                                t1 = t1R.next()
                                t2 = t2R.next()
                                k.op("pool", lambda: Pq.tensor_tensor(out=t1.t[:, 0:N], in0=zg.t[:, 0:N], in1=cs.t[:, 0, 0:N], op=ALU.mult), reads=zg.b + cs.b, writes=t1.b)
                                k.op("dve", lambda: V.tensor_tensor(out=t2.t[:, 0:N], in0=zr.t[:, 0:N], in1=cs.t[:, 1, 0:N], op=ALU.mult), reads=[zr.b] + cs.b, writes=t2.b)
                                k.op("pool", lambda: Pq.tensor_tensor(out=dest, in0=t1.t[:, 0:N], in1=t2.t[:, 0:N], op=ALU.add), reads=t1.b + t2.b, writes=destb)

                            n = len(ocs)
                            for s_ in range(n + 2):
                                if s_ < n:
                                    stage0(ocs[s_])
                                if 0 <= s_ - 1 < n:
                                    stage1(ocs[s_ - 1])
                                if 0 <= s_ - 2 < n:
                                    stage2(ocs[s_ - 2])
                            if ti + 1 < len(TILES):
                                normA(ti + 1)
                            for j in range(N // 128):
                                st = st0 + j
                                if full:
                                    ps2 = PSUM.next()
                                    for ab in range(2):
                                        for fc in range(2):
                                            q4 = ab * 2 + fc
                                            k.op("pe", lambda: nc.tensor.matmul(ps2.t[:, q4 * 128:(q4 + 1) * 128], lhsT=uT[fc].t[:, j * 128:(j + 1) * 128],
                                                                                rhs=abd.t[:, l, q4, :], start=True, stop=True), reads=uT[fc].b, writes=[ps2.b])
                                    k.op("dve", lambda: V.tensor_copy(out=UAB.t[:, st, :], in_=ps2.t[:, 0:512]), reads=[ps2.b], writes=[UAB.b[ti]])
                                ps3 = PSUM.next()
                                for dc in range(8):
                                    k.op("pe", lambda: nc.tensor.matmul(ps3.t[:, 0:384], lhsT=hT.t[:, dc, j * 128:(j + 1) * 128], rhs=win.t[:, dc, 896:1280],
                                                                        start=(dc == 0), stop=(dc == 7)), reads=[hT.b[dc]] + win.b, writes=[ps3.b])
                                k.op("act", lambda: A.activation(out=VP.t[:, st, 0:256].rearrange("p (a n) -> p a n", a=2)[:, :, 0:64],
                                                                 in_=ps3.t[:, 0:128].rearrange("p (a n) -> p a n", a=2), func=AF.Copy), reads=[ps3.b], writes=[VP.b[ti]])
                                k.op("dve", lambda: V.tensor_copy(out=VP.t[:, st, 192:448], in_=ps3.t[:, 128:384]), reads=[ps3.b], writes=[VP.b[ti]])
                        if debug and l == 0 and b == 0:
                            k.dma("sp", dbg["qt"][:, :, :], QT.t[:], reads=[x for r in QTb for x in r])
                            k.dma("sp", dbg["kt"][:, :], KT.t[:], reads=KT.b)
                            k.dma("sp", dbg["vp"][:, :, :], VP.t[:], reads=VP.b)
                            k.dma("sp", dbg["uab"][:, :, :], UAB.t[:], reads=UAB.b)
                        k.barrier()
                    if stop_after == "A":
                        k.barrier()
                        return nc
                    with ExitStack() as ph:
                        wout = sb("wout", [128, 8, 1024], BF16, stack=ph)
                        load_w(wout, d["w_out"][l], 1024, 8)
                        PTR = Rot([sb(f"PT{i}", [128, 1024], BF16, stack=ph) for i in range(3)])
                        DFR = Rot([sb(f"DF{i}", [128, 2, 4, 512], BF16, stack=ph) for i in range(2)])
                        MXP = sb("MXP", [128, 2, 512], BF16, n=2, stack=ph)
                        MXF = sb("MXF", [128, 2, 512], BF16, n=2, stack=ph)
                        LsR = Rot([sb(f"Ls{i}", [128, 512], F32, stack=ph) for i in range(2)])
                        OsR = Rot([sb(f"Os{i}", [128, 512], F32, stack=ph) for i in range(2)])
                        dTR = Rot([sb(f"dT{i}", [128, 512], BF16, stack=ph) for i in range(2)])
                        SR = Rot([(psum_all[:, 0:1024], [PSB[0], PSB[1]]), (psum_all[:, 1024:2048], [PSB[2], PSB[3]])])
                        PS4 = Rot([PS(i) for i in range(4, 8)])
                        for ti, (T0, N, isctx) in enumerate(TILES):
                            if isctx and l == 1:
                                continue
                            col = 2 if isctx else b
                            st0 = T0 // 128
                            src, srcb = xsrc(l, b, ti)

                            def load_xt():
                                for c8 in range(8):
                                    k.dma("sp", XT.t[:, c8, 0:N], src[:, c8, :], reads=[srcb] if srcb else [], writes=[XT.b[c8]])
                            kts = [0, 1] if isctx else list(range(NST))
                            steps = [(c, idx, kt) for c in range(4) for idx, kt in enumerate(kts)]

                            def emit_qk(c, kt):
                                kti = 0 if kt < 2 else 1 + (kt - 2) // 4
                                St, Sb = SR.next()
                                k.op("pe", lambda: nc.tensor.matmul(St[:, 0:N], lhsT=KT.t[0:64, kt * 128:(kt + 1) * 128], rhs=QT.t[0:64, c, T0:T0 + N],
                                                                    start=True, stop=True), reads=[KT.b[kti], QTb[c][ti]], writes=Sb)
                                k.op("pe", lambda: nc.tensor.matmul(St[:, 512:512 + N], lhsT=KT.t[64:128, kt * 128:(kt + 1) * 128], rhs=QT.t[64:128, c, T0:T0 + N],
                                                                    start=True, stop=True), reads=[KT.b[kti], QTb[c][ti]], writes=Sb)
                                return St, Sb
                            OL = {}
                            fq = []
                            yp = None
                            if not isctx:
                                lp0 = T0 - CTX
                                yp = [PS(6), PS(7)]
                                dfs = {}

                                def f_dma(ltg):
                                    df = DFR.next()
                                    dfs[ltg] = df
                                    k.dma("sp", df.t[:, 0, :, :], d["dft_c"][ltg * 4:(ltg + 1) * 4].rearrange("l p n -> p l n")[:, :, lp0:lp0 + 512], writes=df.b)
                                    k.dma("sp", df.t[:, 1, :, :], d["dft_s"][ltg * 4:(ltg + 1) * 4].rearrange("l p n -> p l n")[:, :, lp0:lp0 + 512], writes=df.b)

                                def f_mm(ltg, l4, csi):
                                    lt = ltg * 4 + l4
                                    lti = 1 + lt // 4
                                    df = dfs[ltg]
                                    for fc in range(2):
                                        k.op("pe", lambda: nc.tensor.matmul(yp[fc].t[:, 0:512], lhsT=UAB.t[:, 2 + lt, (csi * 2 + fc) * 128:(csi * 2 + fc + 1) * 128], rhs=df.t[:, csi, l4, :],
                                                                            start=(lt == 0 and csi == 0), stop=(lt == 31 and csi == 1)), reads=df.b + [UAB.b[lti]], writes=[yp[fc].b])
                                f_dma(0)
                                f_dma(1)
                                for ltg in range(8):
                                    for l4 in range(4):
                                        for csi in range(2):
                                            fq.append((f_mm, (ltg, l4, csi)))
                                    if ltg + 2 < 8:
                                        fq.append((f_dma, (ltg + 2,)))
                            load_xt()
                            fqi = 0
                            pend = emit_qk(steps[0][0], steps[0][2])
                            for si, (c, idx, kt) in enumerate(steps):
                                kti = 0 if kt < 2 else 1 + (kt - 2) // 4
                                St, Sb = pend
                                if idx == 0:
                                    OL[c] = (PS(4), PS(5))
                                O, Lp = OL[c]
                                PT = PTR.next()
                                if N == 512:
                                    k.op("act", lambda: A.activation(out=PT.t[:, :], in_=St[:, :], func=AF.Exp, scale=0.125), reads=Sb, writes=PT.b)
                                else:
                                    k.op("act", lambda: A.activation(out=PT.t[:, :].rearrange("p (a n) -> p a n", a=2)[:, :, 0:N],
                                                                     in_=St.rearrange("p (a n) -> p a n", a=2)[:, :, 0:N], func=AF.Exp, scale=0.125), reads=Sb, writes=PT.b)
                                if si + 1 < len(steps):
                                    pend = emit_qk(steps[si + 1][0], steps[si + 1][2])
                                tgt = len(fq) if si == len(steps) - 1 else (len(fq) * (si + 1)) // len(steps)
                                while fqi < tgt:
                                    fn, args = fq[fqi]
                                    fn(*args)
                                    fqi += 1
                                first, last = idx == 0, idx == len(kts) - 1
                                rd = PT.b + [VP.b[kti]]
                                k.op("pe", lambda: nc.tensor.matmul(O.t[:, 0:N], lhsT=VP.t[:, kt, 0:128], rhs=PT.t[:, 0:N], start=first, stop=last), reads=rd, writes=[O.b])
                                k.op("pe", lambda: nc.tensor.matmul(Lp.t[:, 0:N], lhsT=VP.t[:, kt, 64:192], rhs=PT.t[:, 512:512 + N], start=first, stop=last), reads=rd, writes=[Lp.b])
                                if last:
                                    Ls = LsR.next()
                                    Os = OsR.next()
                                    k.op("dve", lambda: V.tensor_copy(out=Os.t[0:64, 0:N], in_=O.t[0:64, 0:N]), reads=[O.b], writes=Os.b)
                                    k.op("dve", lambda: V.tensor_copy(out=Ls.t[0:64, 0:N], in_=O.t[64:128, 0:N]), reads=[O.b], writes=Ls.b)
                                    k.op("dve", lambda: V.tensor_copy(out=Os.t[64:128, 0:N], in_=Lp.t[64:128, 0:N]), reads=[Lp.b], writes=Os.b)
                                    k.op("dve", lambda: V.tensor_copy(out=Ls.t[64:128, 0:N], in_=Lp.t[0:64, 0:N]), reads=[Lp.b], writes=Ls.b)
                                    k.op("dve", lambda: V.reciprocal(out=Ls.t[:, 0:N], in_=Ls.t[:, 0:N]), reads=Ls.b, writes=Ls.b)
                                    k.op("pool", lambda: Pq.tensor_tensor(out=QT.t[:, c, T0:T0 + N], in0=Os.t[:, 0:N], in1=Ls.t[:, 0:N], op=ALU.mult),
                                         reads=Ls.b + Os.b, writes=[QTb[c][ti]])
                            if not isctx:
                                assert fqi == len(fq)
                                for fc in range(2):
                                    k.op("dve", lambda: V.tensor_copy(out=MXF.t[:, fc, 0:N], in_=yp[fc].t[:, 0:N]), reads=[yp[fc].b], writes=[MXF.b[fc]])
                            seq0 = 0 if isctx else 2
                            nsub = 2 if isctx else 32
                            for pc in range(2):
                                dps = PS4.next()
                                for j in range(N // 128):
                                    sj = st0 - seq0 + j
                                    lst = []
                                    if sj > 0:
                                        lst.append((sj - 1, 0))
                                    lst.append((sj, 2 if sj == 0 else (4 if sj == nsub - 1 else 3)))
                                    if sj < nsub - 1:
                                        lst.append((sj + 1, 1))
                                    for idx, (sk, var) in enumerate(lst):
                                        skt = seq0 + sk
                                        skti = 0 if skt < 2 else 1 + (skt - 2) // 4
                                        for gg in range(2):
                                            g = 2 * pc + gg
                                            bo = (g * 5 + var) * 128
                                            k.op("pe", lambda: nc.tensor.matmul(dps.t[gg * 64:(gg + 1) * 64, j * 128:(j + 1) * 128], lhsT=VP.t[:, skt, 192 + g * 64:192 + (g + 1) * 64],
                                                                                rhs=bands.t[:, bo:bo + 128], start=(idx == 0), stop=(idx == len(lst) - 1)),
                                                 reads=[VP.b[skti]], writes=[dps.b])
                                dT = dTR.next()
                                k.op("dve", lambda: V.tensor_copy(out=dT.t[:, 0:N], in_=dps.t[:, 0:N]), reads=[dps.b], writes=dT.b)
                                yps = PS4.next()
                                k.op("pe", lambda: nc.tensor.matmul(yps.t[:, 0:N], lhsT=wpb.t[:, l, pc, :], rhs=dT.t[:, 0:N], start=True, stop=True), reads=dT.b, writes=[yps.b])
                                k.op("dve", lambda: V.tensor_scalar(out=MXP.t[:, pc, 0:N], in0=yps.t[:, 0:N], scalar1=pscale.t[:, l, pc:pc + 1], scalar2=None, op0=ALU.mult),
                                     reads=[yps.b], writes=[MXP.b[pc]])
                            if isctx:
                                for fc in range(2):
                                    yps = PS4.next()
                                    n = 0
                                    for lt in range(2):
                                        for csi in range(2):
                                            k.op("pe", lambda: nc.tensor.matmul(yps.t[:, 0:N], lhsT=UAB.t[:, lt, (csi * 2 + fc) * 128:(csi * 2 + fc + 1) * 128], rhs=d256.t[:, csi, lt, :],
                                                                                start=(n == 0), stop=(n == 3)), reads=[UAB.b[0]], writes=[yps.b])
                                            n += 1
                                    k.op("dve", lambda: V.tensor_copy(out=MXF.t[:, fc, 0:N], in_=yps.t[:, 0:N]), reads=[yps.b], writes=[MXF.b[fc]])
                            if debug and l == 0 and b == 0:
                                k.dma("sp", dbg["mx"][:, 0:4, T0:T0 + N], QT.t[:, :, T0:T0 + N], reads=[QTb[c][ti] for c in range(4)])
                                k.dma("sp", dbg["mx"][:, 4:6, T0:T0 + N], MXP.t[:, :, 0:N], reads=MXP.b)
                                k.dma("sp", dbg["mx"][:, 6:8, T0:T0 + N], MXF.t[:, :, 0:N], reads=MXF.b)
                            ssp = PS4.next()

                            def ss_mm(oc):
                                k.op("pe", lambda: nc.tensor.matmul(ssp.t[:, 0:N], lhsT=ones.t[:], rhs=sq.t[:, oc, 0:N], start=(oc == 0), stop=(oc == 7)),
                                     reads=[sq.b[oc]], writes=[ssp.b])
                            psr = Rot([p for p in PS4.items if p is not ssp])
                            for oc in range(8):
                                ps = psr.next()
                                for mc in range(8):
                                    if mc < 4:
                                        rhs, rb = QT.t[:, mc, T0:T0 + N], [QTb[mc][ti]]
                                    elif mc < 6:
                                        rhs, rb = MXP.t[:, mc - 4, 0:N], [MXP.b[mc - 4]]
                                    else:
                                        rhs, rb = MXF.t[:, mc - 6, 0:N], [MXF.b[mc - 6]]
                                    k.op("pe", lambda: nc.tensor.matmul(ps.t[:, 0:N], lhsT=wout.t[:, mc, oc * 128:(oc + 1) * 128], rhs=rhs, start=(mc == 0), stop=(mc == 7)),
                                         reads=rb + wout.b, writes=[ps.b])
                                k.op("dve", lambda: V.scalar_tensor_tensor(out=XT.t[:, oc, 0:N], in0=ps.t[:, 0:N], scalar=modT.t[:, l, 16 + oc, col:col + 1], in1=XT.t[:, oc, 0:N],
                                                                           op0=ALU.mult, op1=ALU.add), reads=[ps.b, XT.b[oc]], writes=[XT.b[oc]])
                                k.op("pool", lambda: Pq.tensor_tensor(out=sq.t[:, oc, 0:N], in0=XT.t[:, oc, 0:N], in1=XT.t[:, oc, 0:N], op=ALU.mult), reads=[XT.b[oc]], writes=[sq.b[oc]])
                                k.dma("pool", x1s[b].rearrange("c p n -> p c n")[:, oc, T0:T0 + N], XT.t[:, oc, 0:N], reads=[XT.b[oc]], writes=[x1b[b][ti]])
                                if oc > 0:
                                    ss_mm(oc - 1)
                            ss_mm(7)
                            k.op("act", lambda: A.activation(out=lnv.t[:, 0:N], in_=ssp.t[:, 0:N], func=AF.Ln, scale=1.0 / D, bias=EPS), reads=[ssp.b], writes=lnv.b)
                            k.op("act", lambda: A.activation(out=rstd.t[:, 0:N], in_=lnv.t[:, 0:N], func=AF.Exp, scale=-0.5), reads=lnv.b, writes=rstd.b)
                            k.dma("pool", rs2[b][:, T0:T0 + N], rstd.t[:, 0:N], reads=rstd.b, writes=[rsb[b][ti]])
                        k.barrier()
                    if stop_after == "BC":
                        return nc
            with ExitStack() as ph:
                wgu = sb("wgu", [128, 8, 2 * DFF], BF16, n=12, stack=ph)
                wdn = sb("wdn", [128, 22, 1024], BF16, n=6, stack=ph)
                gsrc = d["w_gu"][l].rearrange("c p n -> p c n")
                dsrc = d["w_dn"][l].rearrange("c p n -> p c n")
                for g in range(6):
                    c0, c1 = g * 512, min(DFF, (g + 1) * 512)
                    k.dma("pool", wgu.t[:, :, c0:c1], gsrc[:, :, c0:c1], writes=[wgu.b[2 * g]])
                    k.dma("pool", wgu.t[:, :, DFF + c0:DFF + c1], gsrc[:, :, DFF + c0:DFF + c1], writes=[wgu.b[2 * g + 1]])
                for g in range(6):
                    r0, r1 = g * 4, min(22, (g + 1) * 4)
                    k.dma("pool", wdn.t[:, r0:r1, :], dsrc[:, r0:r1, :], writes=[wdn.b[g]])
                XR = Rot([sb(f"XF{i}", [128, 8, FT], F32, n=8, stack=ph) for i in range(2)])
                RR = Rot([sb(f"RF{i}", [128, FT], F32, stack=ph) for i in range(2)])
                fT = sb("fT", [128, 8, FT], BF16, n=8, stack=ph)
                aT = sb("aT", [128, 22, FT], BF16, n=22, stack=ph)
                xnR = Rot([sb(f"xn{i}", [128, FT], F32, stack=ph) for i in range(2)])
                sgR = Rot([sb(f"sg{i}", [128, FT], F32, stack=ph) for i in range(2)])
                items = [(b, T0, N, isctx) for b in range(NB) for (T0, N, isctx) in FTILES if not (isctx and l == 1)]
                XRs = {}

                def f_load(i):
                    b, T0, N, isctx = items[i]
                    pti = 0 if isctx else 1 + (T0 - CTX) // 512
                    X = XR.next()
                    R = RR.next()
                    XRs[i] = (X, R)
                    k.dma("sp", X.t[:, :, 0:N], x1s[b].rearrange("c p n -> p c n")[:, :, T0:T0 + N], reads=[x1b[b][pti]], writes=X.b)
                    k.dma("sp", R.t[:, 0:N], rs2[b][:, T0:T0 + N], reads=[rsb[b][pti]], writes=R.b)

                def f_norm(i):
                    b, T0, N, isctx = items[i]
                    col = 2 if isctx else b
                    X, R = XRs[i]
                    for c in range(8):
                        xn = xnR.next()
                        k.op("pool", lambda: Pq.tensor_tensor(out=xn.t[:, 0:N], in0=X.t[:, c, 0:N], in1=R.t[:, 0:N], op=ALU.mult), reads=[X.b[c]] + R.b, writes=xn.b)
                        k.op("dve", lambda: V.tensor_scalar(out=fT.t[:, c, 0:N], in0=xn.t[:, 0:N], scalar1=gsc2.t[:, l, c, col:col + 1],
                                                            scalar2=modT.t[:, l, 24 + c, col:col + 1], op0=ALU.mult, op1=ALU.add), reads=xn.b, writes=[fT.b[c]])

                def f_gateup(i):
                    b, T0, N, isctx = items[i]
                    for j in range(22):
                        gps = PSUM.next()
                        ups = PSUM.next()
                        for dc in range(8):
                            k.op("pe", lambda: nc.tensor.matmul(gps.t[:, 0:N], lhsT=wgu.t[:, dc, j * 128:(j + 1) * 128], rhs=fT.t[:, dc, 0:N], start=(dc == 0), stop=(dc == 7)),
                                 reads=[fT.b[dc], wgu.b[2 * (j // 4)]], writes=[gps.b])
                        for dc in range(8):
                            k.op("pe", lambda: nc.tensor.matmul(ups.t[:, 0:N], lhsT=wgu.t[:, dc, DFF + j * 128:DFF + (j + 1) * 128], rhs=fT.t[:, dc, 0:N], start=(dc == 0), stop=(dc == 7)),
                                 reads=[fT.b[dc], wgu.b[2 * (j // 4) + 1]], writes=[ups.b])
                        sg = sgR.next()
                        k.op("act", lambda: A.activation(out=sg.t[:, 0:N], in_=gps.t[:, 0:N], func=AF.Silu), reads=[gps.b], writes=sg.b)
                        k.op("dve", lambda: V.tensor_tensor(out=aT.t[:, j, 0:N], in0=ups.t[:, 0:N], in1=sg.t[:, 0:N], op=ALU.mult), reads=[ups.b] + sg.b, writes=[aT.b[j]])

                def f_down(i):
                    b, T0, N, isctx = items[i]
                    col = 2 if isctx else b
                    pti = 0 if isctx else 1 + (T0 - CTX) // 512
                    X, R = XRs[i]
                    for oc in range(8):
                        dps = PSUM.next()
                        for j in range(22):
                            k.op("pe", lambda: nc.tensor.matmul(dps.t[:, 0:N], lhsT=wdn.t[:, j, oc * 128:(oc + 1) * 128], rhs=aT.t[:, j, 0:N], start=(j == 0), stop=(j == 21)),
                                 reads=[aT.b[j], wdn.b[j // 4]], writes=[dps.b])
                        k.op("dve", lambda: V.scalar_tensor_tensor(out=X.t[:, oc, 0:N], in0=dps.t[:, 0:N], scalar=modT.t[:, l, 40 + oc, col:col + 1], in1=X.t[:, oc, 0:N],
                                                                   op0=ALU.mult, op1=ALU.add), reads=[dps.b, X.b[oc]], writes=[X.b[oc]])
                    if l == 0:
                        k.dma("pool", x2s[b].rearrange("c p n -> p c n")[:, :, T0:T0 + N], X.t[:, :, 0:N], reads=X.b, writes=[x2b[b][pti]])
                    else:
                        k.dma("pool", yT[b].rearrange("c p n -> p c n")[:, :, T0 - CTX:T0 - CTX + N], X.t[:, :, 0:N], reads=X.b)

                f_load(0)
                f_norm(0)
                for i in range(len(items)):
                    if i + 1 < len(items):
                        f_load(i + 1)
                    f_gateup(i)
                    if i + 1 < len(items):
                        f_norm(i + 1)
                    f_down(i)
                k.barrier()
            if stop_after == "F":
                return nc
        k.barrier()
    return nc


_CONST = {}


def _constants():
    if _CONST:
        return _CONST
    n_freq = 16
    freqs = (10000.0 ** (-np.arange(n_freq, dtype=np.float32) / n_freq)).astype(np.float32)
    t = np.arange(T)
    pos = [(t // 64).astype(np.float32), (t % 64).astype(np.float32)]
    cos = np.zeros((128, T), np.float32)
    sin = np.zeros((128, T), np.float32)
    prot = np.zeros((128, 128), np.float32)
    for p in range(128):
        dd = p % 64
        s, half, f = dd // 32, (dd % 32) // 16, dd % 16
        ang = (pos[s] * freqs[f]).astype(np.float32)
        cos[p] = np.cos(ang)
        sin[p] = np.sin(ang) * (-1.0 if half == 0 else 1.0)
        partner = p + 16 if half == 0 else p - 16
        prot[partner, p] = 1.0
    _CONST["rope_cos"] = cos
    _CONST["rope_sin"] = sin
    _CONST["prot"] = prot.astype(NPBF)
    c64 = np.arange(64)
    ang = 2 * np.pi * np.outer(c64, c64) / 64.0
    cc, sc = np.cos(ang), np.sin(ang)
    z = np.zeros((64, 64))
    _CONST["ccbd"] = np.block([[cc, z], [z, cc]]).astype(np.float32)
    _CONST["scbd"] = np.block([[sc, z], [z, sc]]).astype(np.float32)
    li = np.arange(T, dtype=np.int64)
    m = (np.outer(li, li) % T).astype(np.float64) * (2 * np.pi / T)
    _CONST["dft_c"] = (np.cos(m) / 512.0).astype(np.float32).astype(NPBF).reshape(32, 128, T)
    _CONST["dft_s"] = (-np.sin(m) / 512.0).astype(np.float32).astype(NPBF).reshape(32, 128, T)
    l2 = np.arange(CTX, dtype=np.int64)
    m2 = (np.outer(l2, l2) % CTX).astype(np.float64) * (2 * np.pi / CTX)
    c2 = (np.cos(m2) / 128.0).reshape(2, 128, CTX)
    s2 = (-np.sin(m2) / 128.0).reshape(2, 128, CTX)
    _CONST["dft256"] = np.stack([c2, s2], 0).transpose(2, 0, 1, 3).astype(np.float32).astype(NPBF)
    Ls = 384
    bands = np.zeros((128, 4, 5, 128), np.float32)
    tt = np.arange(Ls)
    for g, w in enumerate(POOLW):
        lo = np.clip(tt - w // 2, 0, Ls)
        hi = np.clip(tt + w - w // 2, 0, Ls)
        Dm = np.zeros((Ls, Ls), np.float64)
        for tp in range(Ls):
            Dm[lo[tp]:hi[tp], tp] = 1.0 / (hi[tp] - lo[tp])
            Dm[tp, tp] -= 1.0
        bands[:, g, 0] = Dm[0:128, 128:256]
        bands[:, g, 1] = Dm[256:384, 128:256]
        bands[:, g, 2] = Dm[0:128, 0:128]
        bands[:, g, 3] = Dm[128:256, 128:256]
        bands[:, g, 4] = Dm[256:384, 256:384]
    _CONST["bands"] = bands.reshape(128, 2560).astype(NPBF)
    return _CONST


def _bd(a, b):
    z = np.zeros((64, 64), np.float32)
    return np.block([[a, z], [z, b]])


def _prep_shared(inp):
    f = lambda a: np.ascontiguousarray(np.asarray(a, dtype=np.float32))
    sh = dict(_constants())
    sh["w_ada"] = f(inp["w_ada"]).reshape(2, 8, 128, 6144)
    sh["b_ada"] = f(f(inp["b_ada"]).reshape(2, 48, 128).transpose(0, 2, 1))
    sh["g_mix"] = f(f(inp["g_mix"]).reshape(2, 8, 128).transpose(0, 2, 1))
    sh["g_ffn"] = f(f(inp["g_ffn"]).reshape(2, 8, 128).transpose(0, 2, 1))
    w_in = f(inp["w_in"])
    qcols = []
    for c in range(4):
        qcols += list(range(64 * c, 64 * c + 64)) + list(range(64 * (c + 4), 64 * (c + 4) + 64))
    cols = qcols + list(range(512, 640)) + list(range(1024, 1280)) + list(range(640, 768)) + list(range(768, 1024))
    sh["w_in"] = f(w_in[:, :, cols]).reshape(2, 8, 128, 1280)
    qg, kg = f(inp["q_gain"]), f(inp["k_gain"])
    sh["qk_gain"] = f(np.stack([np.tile(qg, (1, 2)), np.tile(kg, (1, 2))], axis=-1))
    wp, wf = f(inp["w_pool"]), f(inp["w_four"])
    sh["w_pool"] = f(np.stack([np.stack([_bd(wp[l, 2 * pc], wp[l, 2 * pc + 1]) for pc in range(2)], 1) for l in range(2)], 0))
    sh["w_four"] = f(np.stack([np.stack([_bd(wf[l, 2 * pc], wf[l, 2 * pc + 1]) for pc in range(2)], 1) for l in range(2)], 0))
    sh["pool_scale"] = f(f(inp["pool_scale"]).reshape(2, 2, 128).transpose(0, 2, 1))
    w_out = f(inp["w_out"])
    rows = qcols + list(range(512, 1024))
    sh["w_out"] = f(w_out[:, rows, :]).reshape(2, 8, 128, 1024)
    sh["w_gu"] = f(inp["w_gate_up"]).reshape(2, 8, 128, 2 * DFF)
    sh["w_dn"] = f(inp["w_down"]).reshape(2, 22, 128, 1024)
    return sh


def _prep_core(inp, i):
    f = lambda a: np.ascontiguousarray(np.asarray(a, dtype=np.float32))
    b0 = NB * i
    x = np.asarray(inp["x"])[b0:b0 + NB]
    ctx = np.asarray(inp["ctx"])[b0:b0 + NB]
    c = np.asarray(inp["c"])[b0:b0 + NB]
    cctx = np.asarray(inp["c_ctx"])
    m = {}
    m["xT"] = f(x.transpose(0, 2, 1)).reshape(NB, 8, 128, T)
    m["ctxT"] = f(ctx.transpose(0, 2, 1)).reshape(NB, 8, 128, CTX)
    cc = np.stack([c[0], c[1], cctx, cctx], axis=1)
    m["cc"] = f(cc.reshape(8, 128, 4).transpose(1, 0, 2))
    return m


_NC_CACHE = {}


def kernel(**inputs):
    if "nc" not in _NC_CACHE:
        _NC_CACHE["nc"] = build_program()
    nc = _NC_CACHE["nc"]
    sh = _prep_shared(inputs)
    in_maps = []
    for i in range(NCORES):
        m = dict(sh)
        m.update(_prep_core(inputs, i))
        in_maps.append(m)
    res = run_bass_kernel_spmd(nc, in_maps, core_ids=list(range(NCORES)))
    outs = []
    for i in range(NCORES):
        y = np.asarray(res.results[i]["yT"]).reshape(NB, D, T)
        outs.append(y.transpose(0, 2, 1))
    return np.ascontiguousarray(np.concatenate(outs, axis=0).astype(np.float32))
```

```python
import numpy as np
import ml_dtypes
from contextlib import ExitStack
import concourse.bass as bass
import concourse.mybir as mybir
from concourse.bass_utils import run_bass_kernel_spmd

F32 = mybir.dt.float32
BF16 = mybir.dt.bfloat16
AF = mybir.ActivationFunctionType
ALU = mybir.AluOpType
NPBF = ml_dtypes.bfloat16

NCORES = 8
NB = 2
T = 4096
CTX = 256
TT = T + CTX
D = 1024
DFF = 2816
EPS = 1e-6
POOLW = (2, 4, 8, 16)
TILES = [(0, 256, True)] + [(256 + 512 * i, 512, False) for i in range(8)]
FT = 256
FTILES = [(0, 256, True)] + [(256 + FT * i, FT, False) for i in range(T // FT)]
NST = TT // 128


class Buf:
    __slots__ = ("w", "r")

    def __init__(self):
        self.w = None
        self.r = {}


class TB:
    def __init__(self, t, n=1):
        self.t = t
        self.b = [Buf() for _ in range(n)]


class Rot:
    def __init__(self, items):
        self.items = items
        self.i = 0

    def next(self):
        it = self.items[self.i]
        self.i = (self.i + 1) % len(self.items)
        return it


NDS = 8


class KB:
    def __init__(self, nc, es):
        self.nc = nc
        self.E = dict(pe=nc.tensor, act=nc.scalar, dve=nc.vector, pool=nc.gpsimd, sp=nc.sync)
        self.csem = {e: es.enter_context(nc.semaphore("c_" + e)) for e in ["pe", "act", "dve", "pool"]}
        self.cnt = {e: 0 for e in self.csem}
        self.waited = {e: {} for e in self.E}
        self.dsem = {}
        for q in ["sp", "pool"]:
            self.dsem[q] = [[es.enter_context(nc.semaphore(f"d_{q}{i}")), 0] for i in range(NDS)]
        self.drr = {q: 0 for q in self.dsem}
        self.nwait = 0

    def _wait(self, eng, h):
        key, sem, val = h
        if eng == "pe" and key == "c_pe":
            return
        w = self.waited[eng]
        if w.get(key, 0) >= val:
            return
        self.E[eng].wait_ge(sem, val)
        self.nwait += 1
        w[key] = val

    def _deps(self, eng, reads, writes):
        for b in reads:
            if b.w is not None:
                self._wait(eng, b.w)
        for b in writes:
            if b.w is not None:
                self._wait(eng, b.w)
            for h in b.r.values():
                self._wait(eng, h)

    @staticmethod
    def _mark(h, reads, writes):
        for b in reads:
            old = b.r.get(h[0])
            if old is None or old[2] < h[2]:
                b.r[h[0]] = h
        for b in writes:
            b.w = h
            b.r = {}

    def op(self, eng, ins_fn, reads=(), writes=()):
        self._deps(eng, reads, writes)
        ins = ins_fn()
        self.cnt[eng] += 1
        ins.then_inc(self.csem[eng], 1)
        h = ("c_" + eng, self.csem[eng], self.cnt[eng])
        self._mark(h, reads, writes)
        return h

    def dma(self, q, out, in_, reads=(), writes=()):
        self._deps(q, reads, writes)
        idx = self.drr[q]
        self.drr[q] = (idx + 1) % NDS
        slot = self.dsem[q][idx]
        sem, c = slot
        key = f"d_{q}{idx}"
        if c > 0:
            self._wait(q, (key, sem, 16 * c))
        ins = self.E[q].dma_start(out=out, in_=in_)
        ins.then_inc(sem, 16)
        slot[1] = c + 1
        h = (key, sem, 16 * (c + 1))
        self._mark(h, reads, writes)
        return h

    def barrier(self):
        hs = [("c_" + e, self.csem[e], self.cnt[e]) for e in self.csem if self.cnt[e] > 0]
        for q, slots in self.dsem.items():
            for i, (sem, c) in enumerate(slots):
                if c > 0:
                    hs.append((f"d_{q}{i}", sem, 16 * c))
        for eng in self.E:
            for h in hs:
                key, sem, val = h
                w = self.waited[eng]
                if w.get(key, 0) >= val:
                    continue
                self.E[eng].wait_ge(sem, val)
                w[key] = val


def build_program(debug=False, stop_after=None):
    nc = bass.Bass("TRN2", target_bir_lowering=False)
    d = {}

    def din(name, shape, dt):
        d[name] = nc.dram_tensor(name, shape, dt, kind="ExternalInput").ap()

    din("xT", [NB, 8, 128, T], F32)
    din("ctxT", [NB, 8, 128, CTX], F32)
    din("cc", [128, 8, 4], F32)
    din("w_ada", [2, 8, 128, 6144], F32)
    din("b_ada", [2, 128, 48], F32)
    din("g_mix", [2, 128, 8], F32)
    din("g_ffn", [2, 128, 8], F32)
    din("w_in", [2, 8, 128, 1280], F32)
    din("qk_gain", [2, 128, 2], F32)
    din("w_pool", [2, 128, 2, 128], F32)
    din("pool_scale", [2, 128, 2], F32)
    din("w_four", [2, 128, 2, 128], F32)
    din("w_out", [2, 8, 128, 1024], F32)
    din("w_gu", [2, 8, 128, 2 * DFF], F32)
    din("w_dn", [2, 22, 128, 1024], F32)
    din("rope_cos", [128, T], F32)
    din("rope_sin", [128, T], F32)
    din("prot", [128, 128], BF16)
    din("ccbd", [128, 128], F32)
    din("scbd", [128, 128], F32)
    din("dft_c", [32, 128, T], BF16)
    din("dft_s", [32, 128, T], BF16)
    din("dft256", [128, 2, 2, 256], BF16)
    din("bands", [128, 4 * 5 * 128], BF16)
    skind = "ExternalOutput" if debug else "Internal"
    yT = nc.dram_tensor("yT", [NB, 8, 128, T], F32, kind="ExternalOutput").ap()
    x1s = nc.dram_tensor("x1s", [NB, 8, 128, TT], F32, kind=skind).ap()
    x2s = nc.dram_tensor("x2s", [NB, 8, 128, TT], F32, kind=skind).ap()
    rs2 = nc.dram_tensor("rs2", [NB, 128, TT], F32, kind=skind).ap()
    dbg = {}
    if debug:
        dbg["qt"] = nc.dram_tensor("dbg_qt", [128, 4, TT], BF16, kind="ExternalOutput").ap()
        dbg["kt"] = nc.dram_tensor("dbg_kt", [128, TT], BF16, kind="ExternalOutput").ap()
        dbg["vp"] = nc.dram_tensor("dbg_vp", [128, NST, 448], BF16, kind="ExternalOutput").ap()
        dbg["uab"] = nc.dram_tensor("dbg_uab", [128, NST, 512], BF16, kind="ExternalOutput").ap()
        dbg["mod"] = nc.dram_tensor("dbg_mod", [128, 2, 48, 4], F32, kind="ExternalOutput").ap()
        dbg["mx"] = nc.dram_tensor("dbg_mx", [128, 8, TT], BF16, kind="ExternalOutput").ap()

    x1b = [[Buf() for _ in TILES] for _ in range(NB)]
    x2b = [[Buf() for _ in TILES] for _ in range(NB)]
    rsb = [[Buf() for _ in TILES] for _ in range(NB)]
    CONSTB = Buf()

    with ExitStack() as es:
        k = KB(nc, es)

        uid = [0]

        def sb(name, shape, dt, n=1, stack=None):
            uid[0] += 1
            t = (stack or es).enter_context(nc.sbuf_tensor(f"s{uid[0]}_{name}", shape, dt))
            return TB(t, n)

        psum_all = es.enter_context(nc.psum_tensor("psum_all", [128, 4096], F32))
        PSB = [Buf() for _ in range(8)]

        class PS:
            def __init__(self, i):
                self.i = i
                self.t = psum_all[:, i * 512:(i + 1) * 512]
                self.b = PSB[i]

        PSUM = Rot([PS(i) for i in range(8)])

        ones = sb("ones", [128, 128], BF16)
        bones = sb("bones", [128, 128], BF16)
        prot = sb("prot", [128, 128], BF16)
        bands = sb("bands", [128, 2560], BF16)
        d256 = sb("d256", [128, 2, 2, 256], BF16)
        ccbd = sb("ccbd", [128, 128], F32)
        scbd = sb("scbd", [128, 128], F32)
        scT = sb("scT", [128, 8, 4], F32)
        modT = sb("modT", [128, 2, 48, 4], F32)
        bada = sb("bada", [128, 2, 48], F32)
        gmix = sb("gmix", [128, 2, 8], F32)
        gffn = sb("gffn", [128, 2, 8], F32)
        qkg = sb("qkg", [128, 2, 2], F32)
        pscale = sb("pscale", [128, 2, 2], F32)
        wpb = sb("wpb", [128, 2, 2, 128], BF16)
        wf32 = sb("wf32", [128, 2, 2, 128], F32)
        abd = sb("abd", [128, 2, 4, 128], BF16)
        gsc1 = sb("gsc1", [128, 2, 8, 4], F32)
        gsc2 = sb("gsc2", [128, 2, 8, 4], F32)
        tmpm = sb("tmpm", [128, 8, 4], F32)

        V, Pq, A, S_ = nc.vector, nc.gpsimd, nc.scalar, nc.sync
        k.op("dve", lambda: V.memset(ones.t[:], 1.0), writes=ones.b)
        k.op("dve", lambda: V.memset(bones.t[:], 0.0), writes=bones.b)
        k.op("dve", lambda: V.memset(bones.t[0:64, 0:64], 1.0), writes=bones.b)
        k.op("dve", lambda: V.memset(bones.t[64:128, 64:128], 1.0), writes=bones.b)
        k.dma("sp", prot.t[:], d["prot"][:, :], writes=prot.b)
        k.dma("sp", bands.t[:], d["bands"][:, :], writes=bands.b)
        k.dma("sp", d256.t[:], d["dft256"][:, :, :, :], writes=d256.b)
        k.dma("sp", ccbd.t[:], d["ccbd"][:, :], writes=ccbd.b)
        k.dma("sp", scbd.t[:], d["scbd"][:, :], writes=scbd.b)
        k.dma("sp", scT.t[:], d["cc"][:, :, :], writes=scT.b)
        for l in range(2):
            k.dma("sp", bada.t[:, l, :], d["b_ada"][l], writes=bada.b)
            k.dma("sp", gmix.t[:, l, :], d["g_mix"][l], writes=gmix.b)
            k.dma("sp", gffn.t[:, l, :], d["g_ffn"][l], writes=gffn.b)
            k.dma("sp", qkg.t[:, l, :], d["qk_gain"][l], writes=qkg.b)
            k.dma("sp", pscale.t[:, l, :], d["pool_scale"][l], writes=pscale.b)
            k.dma("sp", wf32.t[:, l, :, :], d["w_four"][l], writes=wf32.b)
            k.dma("pool", wpb.t[:, l, :, :], d["w_pool"][l], writes=wpb.b)
        k.op("act", lambda: A.activation(out=scT.t[:], in_=scT.t[:], func=AF.Silu), reads=scT.b, writes=scT.b)
        for l in range(2):
            for ab in range(2):
                for fc in range(2):
                    ps = PSUM.next()
                    cs = ccbd if ab == 0 else scbd
                    k.op("pe", lambda: nc.tensor.matmul(ps.t[:, 0:128], lhsT=cs.t[:], rhs=wf32.t[:, l, fc, :], start=True, stop=True),
                         reads=cs.b + wf32.b, writes=[ps.b])
                    k.op("dve", lambda: V.tensor_copy(out=abd.t[:, l, ab * 2 + fc, :], in_=ps.t[:, 0:128]), reads=[ps.b], writes=abd.b)

        with ExitStack() as ph:
            wa = Rot([sb(f"wa{i}", [128, 8, 512], F32, stack=ph) for i in range(2)])
            for l in range(2):
                for jb in range(12):
                    w = wa.next()
                    k.dma("sp", w.t[:], d["w_ada"][l].rearrange("c p n -> p c n")[:, :, jb * 512:(jb + 1) * 512], writes=w.b)
                    for jj in range(4):
                        j = jb * 4 + jj
                        ps = PSUM.next()
                        for dc in range(8):
                            k.op("pe", lambda: nc.tensor.matmul(ps.t[:, 0:4], lhsT=w.t[:, dc, jj * 128:(jj + 1) * 128], rhs=scT.t[:, dc, :],
                                                                start=(dc == 0), stop=(dc == 7)), reads=w.b + scT.b, writes=[ps.b])
                        k.op("dve", lambda: V.tensor_scalar(out=modT.t[:, l, j, :], in0=ps.t[:, 0:4], scalar1=bada.t[:, l, j:j + 1], scalar2=None,
                                                            op0=ALU.add), reads=[ps.b] + bada.b, writes=modT.b)
                for (gs, g, base) in ((gsc1, gmix, 8), (gsc2, gffn, 32)):
                    k.op("dve", lambda: V.tensor_scalar(out=tmpm.t[:], in0=modT.t[:, l, base:base + 8, :], scalar1=1.0, scalar2=None, op0=ALU.add),
                         reads=modT.b, writes=tmpm.b)
                    for col in range(3):
                        k.op("dve", lambda: V.tensor_tensor(out=gs.t[:, l, :, col], in0=tmpm.t[:, :, col], in1=g.t[:, l, :], op=ALU.mult),
                             reads=tmpm.b + g.b, writes=gs.b)
            if debug:
                k.dma("sp", dbg["mod"][:, :, :, :], modT.t[:], reads=modT.b)
            k.barrier()

        def xsrc(l, b, ti):
            T0, N, isctx = TILES[ti]
            if l == 0:
                if isctx:
                    return d["ctxT"][b].rearrange("c p n -> p c n"), None
                return d["xT"][b].rearrange("c p n -> p c n")[:, :, T0 - CTX:T0 - CTX + N], None
            return x2s[b].rearrange("c p n -> p c n")[:, :, T0:T0 + N], x2b[b][ti]

        def load_w(wt, src, ncol, rows):
            for c0 in range(0, ncol, 1024):
                c1 = min(ncol, c0 + 1024)
                for r0 in range(0, rows, 8):
                    r1 = min(rows, r0 + 8)
                    k.dma("pool", wt.t[:, r0:r1, c0:c1], src.rearrange("c p n -> p c n")[:, r0:r1, c0:c1], writes=wt.b)

        for l in range(2):
            for b in range(NB):
                with ExitStack() as mix:
                    QT = sb("QT", [128, 4, TT], BF16, stack=mix)
                    QTb = [[Buf() for _ in TILES] for _ in range(4)]
                    KT = sb("KT", [128, TT], BF16, n=len(TILES), stack=mix)
                    VP = sb("VP", [128, NST, 448], BF16, n=len(TILES), stack=mix)
                    k.op("dve", lambda: V.memset(VP.t[:, :, 64:128], 1.0), writes=VP.b)
                    UAB = sb("UAB", [128, NST, 512], BF16, n=len(TILES), stack=mix)
                    XT = sb("XT", [128, 8, 512], F32, n=8, stack=mix)
                    sq = sb("sq", [128, 8, 512], BF16, n=8, stack=mix)
                    lnv = sb("lnv", [128, 512], F32, stack=mix)
                    rstd = sb("rstd", [128, 512], F32, stack=mix)
                    with ExitStack() as ph:
                        win = sb("win", [128, 8, 1280], BF16, stack=ph)
                        load_w(win, d["w_in"][l], 1280, 8)
                        hTs = [sb(f"hT{i}", [128, 8, 512], BF16, n=8, stack=ph) for i in range(2)]
                        cs = sb("cs", [128, 2, 512], F32, stack=ph)
                        zsqR = Rot([sb(f"zsq{i}", [128, 512], BF16, stack=ph) for i in range(2)])
                        hrR = Rot([sb(f"hr{i}", [128, 512], F32, stack=ph) for i in range(2)])
                        zgR = Rot([sb(f"zg{i}", [128, 512], BF16, stack=ph) for i in range(2)])
                        t1R = Rot([sb(f"t1{i}", [128, 512], F32, stack=ph) for i in range(2)])
                        t2R = Rot([sb(f"t2{i}", [128, 512], F32, stack=ph) for i in range(1)])
                        uT = [sb(f"uT{i}", [128, 512], BF16, stack=ph) for i in range(2)]

                        def normA(ti):
                            T0, N, isctx = TILES[ti]
                            col = 2 if isctx else b
                            hT = hTs[ti % 2]
                            src, srcb = xsrc(l, b, ti)
                            k.dma("sp", XT.t[:, :, 0:N], src, reads=[srcb] if srcb else [], writes=XT.b)
                            k.op("act", lambda: A.activation(out=sq.t[:, :, 0:N], in_=XT.t[:, :, 0:N], func=AF.Square), reads=XT.b, writes=sq.b)
                            ps = PSUM.next()
                            for c in range(8):
                                k.op("pe", lambda: nc.tensor.matmul(ps.t[:, 0:N], lhsT=ones.t[:], rhs=sq.t[:, c, 0:N], start=(c == 0), stop=(c == 7)),
                                     reads=sq.b, writes=[ps.b])
                            k.op("act", lambda: A.activation(out=lnv.t[:, 0:N], in_=ps.t[:, 0:N], func=AF.Ln, scale=1.0 / D, bias=EPS), reads=[ps.b], writes=lnv.b)
                            k.op("act", lambda: A.activation(out=rstd.t[:, 0:N], in_=lnv.t[:, 0:N], func=AF.Exp, scale=-0.5), reads=lnv.b, writes=rstd.b)
                            for c in range(8):
                                if c % 2 == 0:
                                    k.op("pool", lambda: Pq.tensor_tensor(out=XT.t[:, c, 0:N], in0=XT.t[:, c, 0:N], in1=rstd.t[:, 0:N], op=ALU.mult),
                                         reads=[XT.b[c]] + rstd.b, writes=[XT.b[c]])
                                else:
                                    k.op("dve", lambda: V.tensor_tensor(out=XT.t[:, c, 0:N], in0=XT.t[:, c, 0:N], in1=rstd.t[:, 0:N], op=ALU.mult),
                                         reads=[XT.b[c]] + rstd.b, writes=[XT.b[c]])
                                k.op("act", lambda: A.activation(out=hT.t[:, c, 0:N], in_=XT.t[:, c, 0:N], func=AF.Identity, scale=gsc1.t[:, l, c, col:col + 1],
                                                                 bias=modT.t[:, l, c, col:col + 1]), reads=[XT.b[c]], writes=[hT.b[c]])

                        normA(0)
                        for ti, (T0, N, isctx) in enumerate(TILES):
                            col = 2 if isctx else b
                            full = not (isctx and l == 1)
                            st0 = T0 // 128
                            hT = hTs[ti % 2]
                            if not isctx:
                                k.dma("sp", cs.t[:, 0, 0:N], d["rope_cos"][:, T0 - CTX:T0 - CTX + N], writes=cs.b)
                                k.dma("sp", cs.t[:, 1, 0:N], d["rope_sin"][:, T0 - CTX:T0 - CTX + N], writes=cs.b)
                            ocs = list(range(7)) if full else [4]
                            stt = {}

                            def stage0(oc):
                                ps = PSUM.next()
                                for dc in range(8):
                                    k.op("pe", lambda: nc.tensor.matmul(ps.t[:, 0:N], lhsT=win.t[:, dc, oc * 128:(oc + 1) * 128], rhs=hT.t[:, dc, 0:N],
                                                                        start=(dc == 0), stop=(dc == 7)), reads=[hT.b[dc]] + win.b, writes=[ps.b])
                                stt[oc] = {"ps": ps}

                            def stage1(oc):
                                ps = stt[oc]["ps"]
                                if oc >= 5:
                                    u = uT[oc - 5]
                                    k.op("dve", lambda: V.tensor_copy(out=u.t[:, 0:N], in_=ps.t[:, 0:N]), reads=[ps.b], writes=u.b)
                                    return
                                zsq = zsqR.next()
                                k.op("act", lambda: A.activation(out=zsq.t[:, 0:N], in_=ps.t[:, 0:N], func=AF.Square), reads=[ps.b], writes=zsq.b)
                                hs = PSUM.next()
                                k.op("pe", lambda: nc.tensor.matmul(hs.t[:, 0:N], lhsT=bones.t[:], rhs=zsq.t[:, 0:N], start=True, stop=True),
                                     reads=zsq.b, writes=[hs.b])
                                hr = hrR.next()
                                k.op("act", lambda: A.activation(out=hr.t[:, 0:N], in_=hs.t[:, 0:N], func=AF.Ln, scale=1.0 / 64, bias=EPS), reads=[hs.b], writes=hr.b)
                                k.op("act", lambda: A.activation(out=hr.t[:, 0:N], in_=hr.t[:, 0:N], func=AF.Exp, scale=-0.5), reads=hr.b, writes=hr.b)
                                gcol = 0 if oc < 4 else 1
                                if oc < 4:
                                    dest, destb = QT.t[:, oc, T0:T0 + N], [QTb[oc][ti]]
                                else:
                                    dest, destb = KT.t[:, T0:T0 + N], [KT.b[ti]]
                                stt[oc]["dest"] = (dest, destb)
                                if isctx:
                                    k.op("dve", lambda: V.scalar_tensor_tensor(out=dest, in0=ps.t[:, 0:N], scalar=qkg.t[:, l, gcol:gcol + 1], in1=hr.t[:, 0:N],
                                                                               op0=ALU.mult, op1=ALU.mult), reads=[ps.b] + hr.b, writes=destb)
                                else:
                                    zg = zgR.next()
                                    stt[oc]["zg"] = zg
                                    k.op("dve", lambda: V.scalar_tensor_tensor(out=zg.t[:, 0:N], in0=ps.t[:, 0:N], scalar=qkg.t[:, l, gcol:gcol + 1], in1=hr.t[:, 0:N],
                                                                               op0=ALU.mult, op1=ALU.mult), reads=[ps.b] + hr.b, writes=zg.b)

                            def stage2(oc):
                                if oc >= 5 or isctx:
                                    return
                                zg = stt[oc]["zg"]
                                dest, destb = stt[oc]["dest"]
                                zr = PSUM.next()
                                k.op("pe", lambda: nc.tensor.matmul(zr.t[:, 0:N], lhsT=prot.t[:], rhs=zg.t[:, 0:N], start=True, stop=True), reads=zg.b, writes=[zr.b])
                                t1 = t1R.next()
                                t2 = t2R.next()
                                k.op("pool", lambda: Pq.tensor_tensor(out=t1.t[:, 0:N], in0=zg.t[:, 0:N], in1=cs.t[:, 0, 0:N], op=ALU.mult), reads=zg.b + cs.b, writes=t1.b)
                                k.op("dve", lambda: V.tensor_tensor(out=t2.t[:, 0:N], in0=zr.t[:, 0:N], in1=cs.t[:, 1, 0:N], op=ALU.mult), reads=[zr.b] + cs.b, writes=t2.b)
                                k.op("pool", lambda: Pq.tensor_tensor(out=dest, in0=t1.t[:, 0:N], in1=t2.t[:, 0:N], op=ALU.add), reads=t1.b + t2.b, writes=destb)

                            n = len(ocs)
                            for s_ in range(n + 2):
                                if s_ < n:
                                    stage0(ocs[s_])
                                if 0 <= s_ - 1 < n:
                                    stage1(ocs[s_ - 1])
                                if 0 <= s_ - 2 < n:
                                    stage2(ocs[s_ - 2])
                            if ti + 1 < len(TILES):
                                normA(ti + 1)
                            for j in range(N // 128):
                                st = st0 + j
                                if full:
                                    ps2 = PSUM.next()
                                    for ab in range(2):
                                        for fc in range(2):
                                            q4 = ab * 2 + fc
                                            k.op("pe", lambda: nc.tensor.matmul(ps2.t[:, q4 * 128:(q4 + 1) * 128], lhsT=uT[fc].t[:, j * 128:(j + 1) * 128],
                                                                                rhs=abd.t[:, l, q4, :], start=True, stop=True), reads=uT[fc].b, writes=[ps2.b])
                                    k.op("dve", lambda: V.tensor_copy(out=UAB.t[:, st, :], in_=ps2.t[:, 0:512]), reads=[ps2.b], writes=[UAB.b[ti]])
                                ps3 = PSUM.next()
                                for dc in range(8):
                                    k.op("pe", lambda: nc.tensor.matmul(ps3.t[:, 0:384], lhsT=hT.t[:, dc, j * 128:(j + 1) * 128], rhs=win.t[:, dc, 896:1280],
                                                                        start=(dc == 0), stop=(dc == 7)), reads=[hT.b[dc]] + win.b, writes=[ps3.b])
                                k.op("act", lambda: A.activation(out=VP.t[:, st, 0:256].rearrange("p (a n) -> p a n", a=2)[:, :, 0:64],
                                                                 in_=ps3.t[:, 0:128].rearrange("p (a n) -> p a n", a=2), func=AF.Copy), reads=[ps3.b], writes=[VP.b[ti]])
                                k.op("dve", lambda: V.tensor_copy(out=VP.t[:, st, 192:448], in_=ps3.t[:, 128:384]), reads=[ps3.b], writes=[VP.b[ti]])
                        if debug and l == 0 and b == 0:
                            k.dma("sp", dbg["qt"][:, :, :], QT.t[:], reads=[x for r in QTb for x in r])
                            k.dma("sp", dbg["kt"][:, :], KT.t[:], reads=KT.b)
                            k.dma("sp", dbg["vp"][:, :, :], VP.t[:], reads=VP.b)
                            k.dma("sp", dbg["uab"][:, :, :], UAB.t[:], reads=UAB.b)
                        k.barrier()
                    if stop_after == "A":
                        k.barrier()
                        return nc
                    with ExitStack() as ph:
                        wout = sb("wout", [128, 8, 1024], BF16, stack=ph)
                        load_w(wout, d["w_out"][l], 1024, 8)
                        PTR = Rot([sb(f"PT{i}", [128, 1024], BF16, stack=ph) for i in range(3)])
                        DFR = Rot([sb(f"DF{i}", [128, 2, 4, 512], BF16, stack=ph) for i in range(2)])
                        MXP = sb("MXP", [128, 2, 512], BF16, n=2, stack=ph)
                        MXF = sb("MXF", [128, 2, 512], BF16, n=2, stack=ph)
                        LsR = Rot([sb(f"Ls{i}", [128, 512], F32, stack=ph) for i in range(2)])
                        OsR = Rot([sb(f"Os{i}", [128, 512], F32, stack=ph) for i in range(2)])
                        dTR = Rot([sb(f"dT{i}", [128, 512], BF16, stack=ph) for i in range(2)])
                        SR = Rot([(psum_all[:, 0:1024], [PSB[0], PSB[1]]), (psum_all[:, 1024:2048], [PSB[2], PSB[3]])])
                        PS4 = Rot([PS(i) for i in range(4, 8)])
                        for ti, (T0, N, isctx) in enumerate(TILES):
                            if isctx and l == 1:
                                continue
                            col = 2 if isctx else b
                            st0 = T0 // 128
                            src, srcb = xsrc(l, b, ti)

                            def load_xt():
                                k.dma("sp", XT.t[:, :, 0:N], src, reads=[srcb] if srcb else [], writes=XT.b)
                            kts = [0, 1] if isctx else list(range(NST))
                            steps = [(c, idx, kt) for c in range(4) for idx, kt in enumerate(kts)]

                            def emit_qk(c, kt):
                                kti = 0 if kt < 2 else 1 + (kt - 2) // 4
                                St, Sb = SR.next()
                                k.op("pe", lambda: nc.tensor.matmul(St[:, 0:N], lhsT=KT.t[0:64, kt * 128:(kt + 1) * 128], rhs=QT.t[0:64, c, T0:T0 + N],
                                                                    start=True, stop=True), reads=[KT.b[kti], QTb[c][ti]], writes=Sb)
                                k.op("pe", lambda: nc.tensor.matmul(St[:, 512:512 + N], lhsT=KT.t[64:128, kt * 128:(kt + 1) * 128], rhs=QT.t[64:128, c, T0:T0 + N],
                                                                    start=True, stop=True), reads=[KT.b[kti], QTb[c][ti]], writes=Sb)
                                return St, Sb
                            OL = {}
                            fq = []
                            yp = None
                            if not isctx:
                                lp0 = T0 - CTX
                                yp = [PS(6), PS(7)]
                                dfs = {}

                                def f_dma(ltg):
                                    df = DFR.next()
                                    dfs[ltg] = df
                                    k.dma("sp", df.t[:, 0, :, :], d["dft_c"][ltg * 4:(ltg + 1) * 4].rearrange("l p n -> p l n")[:, :, lp0:lp0 + 512], writes=df.b)
                                    k.dma("sp", df.t[:, 1, :, :], d["dft_s"][ltg * 4:(ltg + 1) * 4].rearrange("l p n -> p l n")[:, :, lp0:lp0 + 512], writes=df.b)

                                def f_mm(ltg, l4, csi):
                                    lt = ltg * 4 + l4
                                    lti = 1 + lt // 4
                                    df = dfs[ltg]
                                    for fc in range(2):
                                        k.op("pe", lambda: nc.tensor.matmul(yp[fc].t[:, 0:512], lhsT=UAB.t[:, 2 + lt, (csi * 2 + fc) * 128:(csi * 2 + fc + 1) * 128], rhs=df.t[:, csi, l4, :],
                                                                            start=(lt == 0 and csi == 0), stop=(lt == 31 and csi == 1)), reads=df.b + [UAB.b[lti]], writes=[yp[fc].b])
                                f_dma(0)
                                f_dma(1)
                                for ltg in range(8):
                                    for l4 in range(4):
                                        for csi in range(2):
                                            fq.append((f_mm, (ltg, l4, csi)))
                                    if ltg + 2 < 8:
                                        fq.append((f_dma, (ltg + 2,)))
                            load_xt()
                            fqi = 0
                            cumw = []
                            acc_w = 0
                            for (c_, idx_, kt_) in steps:
                                acc_w += 4 if (idx_ == 0 and c_ > 0) else 1
                                cumw.append(acc_w)
                            pend = emit_qk(steps[0][0], steps[0][2])
                            for si, (c, idx, kt) in enumerate(steps):
                                kti = 0 if kt < 2 else 1 + (kt - 2) // 4
                                St, Sb = pend
                                if idx == 0:
                                    OL[c] = (PS(4), PS(5))
                                O, Lp = OL[c]
                                PT = PTR.next()
                                if N == 512:
                                    k.op("act", lambda: A.activation(out=PT.t[:, :], in_=St[:, :], func=AF.Exp, scale=0.125), reads=Sb, writes=PT.b)
                                else:
                                    k.op("act", lambda: A.activation(out=PT.t[:, :].rearrange("p (a n) -> p a n", a=2)[:, :, 0:N],
                                                                     in_=St.rearrange("p (a n) -> p a n", a=2)[:, :, 0:N], func=AF.Exp, scale=0.125), reads=Sb, writes=PT.b)
                                if si + 1 < len(steps):
                                    pend = emit_qk(steps[si + 1][0], steps[si + 1][2])
                                tgt = len(fq) if si == len(steps) - 1 else (len(fq) * cumw[si]) // cumw[-1]
                                while fqi < tgt:
                                    fn, args = fq[fqi]
                                    fn(*args)
                                    fqi += 1
                                first, last = idx == 0, idx == len(kts) - 1
                                rd = PT.b + [VP.b[kti]]
                                k.op("pe", lambda: nc.tensor.matmul(O.t[:, 0:N], lhsT=VP.t[:, kt, 0:128], rhs=PT.t[:, 0:N], start=first, stop=last), reads=rd, writes=[O.b])
                                k.op("pe", lambda: nc.tensor.matmul(Lp.t[:, 0:N], lhsT=VP.t[:, kt, 64:192], rhs=PT.t[:, 512:512 + N], start=first, stop=last), reads=rd, writes=[Lp.b])
                                if last:
                                    Ls = LsR.next()
                                    Os = OsR.next()
                                    k.op("dve", lambda: V.tensor_copy(out=Os.t[0:64, 0:N], in_=O.t[0:64, 0:N]), reads=[O.b], writes=Os.b)
                                    k.op("dve", lambda: V.tensor_copy(out=Ls.t[0:64, 0:N], in_=O.t[64:128, 0:N]), reads=[O.b], writes=Ls.b)
                                    k.op("dve", lambda: V.tensor_copy(out=Os.t[64:128, 0:N], in_=Lp.t[64:128, 0:N]), reads=[Lp.b], writes=Os.b)
                                    k.op("dve", lambda: V.tensor_copy(out=Ls.t[64:128, 0:N], in_=Lp.t[0:64, 0:N]), reads=[Lp.b], writes=Ls.b)
                                    k.op("dve", lambda: V.reciprocal(out=Ls.t[:, 0:N], in_=Ls.t[:, 0:N]), reads=Ls.b, writes=Ls.b)
                                    k.op("pool", lambda: Pq.tensor_tensor(out=QT.t[:, c, T0:T0 + N], in0=Os.t[:, 0:N], in1=Ls.t[:, 0:N], op=ALU.mult),
                                         reads=Ls.b + Os.b, writes=[QTb[c][ti]])
                            if not isctx:
                                assert fqi == len(fq)
                                for fc in range(2):
                                    k.op("dve", lambda: V.tensor_copy(out=MXF.t[:, fc, 0:N], in_=yp[fc].t[:, 0:N]), reads=[yp[fc].b], writes=[MXF.b[fc]])
                            seq0 = 0 if isctx else 2
                            nsub = 2 if isctx else 32
                            for pc in range(2):
                                dps = PS4.next()
                                for j in range(N // 128):
                                    sj = st0 - seq0 + j
                                    lst = []
                                    if sj > 0:
                                        lst.append((sj - 1, 0))
                                    lst.append((sj, 2 if sj == 0 else (4 if sj == nsub - 1 else 3)))
                                    if sj < nsub - 1:
                                        lst.append((sj + 1, 1))
                                    for idx, (sk, var) in enumerate(lst):
                                        skt = seq0 + sk
                                        skti = 0 if skt < 2 else 1 + (skt - 2) // 4
                                        for gg in range(2):
                                            g = 2 * pc + gg
                                            bo = (g * 5 + var) * 128
                                            k.op("pe", lambda: nc.tensor.matmul(dps.t[gg * 64:(gg + 1) * 64, j * 128:(j + 1) * 128], lhsT=VP.t[:, skt, 192 + g * 64:192 + (g + 1) * 64],
                                                                                rhs=bands.t[:, bo:bo + 128], start=(idx == 0), stop=(idx == len(lst) - 1)),
                                                 reads=[VP.b[skti]], writes=[dps.b])
                                dT = dTR.next()
                                k.op("dve", lambda: V.tensor_copy(out=dT.t[:, 0:N], in_=dps.t[:, 0:N]), reads=[dps.b], writes=dT.b)
                                yps = PS4.next()
                                k.op("pe", lambda: nc.tensor.matmul(yps.t[:, 0:N], lhsT=wpb.t[:, l, pc, :], rhs=dT.t[:, 0:N], start=True, stop=True), reads=dT.b, writes=[yps.b])
                                k.op("dve", lambda: V.tensor_scalar(out=MXP.t[:, pc, 0:N], in0=yps.t[:, 0:N], scalar1=pscale.t[:, l, pc:pc + 1], scalar2=None, op0=ALU.mult),
                                     reads=[yps.b], writes=[MXP.b[pc]])
                            if isctx:
                                for fc in range(2):
                                    yps = PS4.next()
                                    n = 0
                                    for lt in range(2):
                                        for csi in range(2):
                                            k.op("pe", lambda: nc.tensor.matmul(yps.t[:, 0:N], lhsT=UAB.t[:, lt, (csi * 2 + fc) * 128:(csi * 2 + fc + 1) * 128], rhs=d256.t[:, csi, lt, :],
                                                                                start=(n == 0), stop=(n == 3)), reads=[UAB.b[0]], writes=[yps.b])
                                            n += 1
                                    k.op("dve", lambda: V.tensor_copy(out=MXF.t[:, fc, 0:N], in_=yps.t[:, 0:N]), reads=[yps.b], writes=[MXF.b[fc]])
                            if debug and l == 0 and b == 0:
                                k.dma("sp", dbg["mx"][:, 0:4, T0:T0 + N], QT.t[:, :, T0:T0 + N], reads=[QTb[c][ti] for c in range(4)])
                                k.dma("sp", dbg["mx"][:, 4:6, T0:T0 + N], MXP.t[:, :, 0:N], reads=MXP.b)
                                k.dma("sp", dbg["mx"][:, 6:8, T0:T0 + N], MXF.t[:, :, 0:N], reads=MXF.b)
                            ssp = PS4.next()

                            def ss_mm(oc):
                                k.op("pe", lambda: nc.tensor.matmul(ssp.t[:, 0:N], lhsT=ones.t[:], rhs=sq.t[:, oc, 0:N], start=(oc == 0), stop=(oc == 7)),
                                     reads=[sq.b[oc]], writes=[ssp.b])
                            psr = Rot([p for p in PS4.items if p is not ssp])
                            for oc in range(8):
                                ps = psr.next()
                                for mc in range(8):
                                    if mc < 4:
                                        rhs, rb = QT.t[:, mc, T0:T0 + N], [QTb[mc][ti]]
                                    elif mc < 6:
                                        rhs, rb = MXP.t[:, mc - 4, 0:N], [MXP.b[mc - 4]]
                                    else:
                                        rhs, rb = MXF.t[:, mc - 6, 0:N], [MXF.b[mc - 6]]
                                    k.op("pe", lambda: nc.tensor.matmul(ps.t[:, 0:N], lhsT=wout.t[:, mc, oc * 128:(oc + 1) * 128], rhs=rhs, start=(mc == 0), stop=(mc == 7)),
                                         reads=rb + wout.b, writes=[ps.b])
                                k.op("dve", lambda: V.scalar_tensor_tensor(out=XT.t[:, oc, 0:N], in0=ps.t[:, 0:N], scalar=modT.t[:, l, 16 + oc, col:col + 1], in1=XT.t[:, oc, 0:N],
                                                                           op0=ALU.mult, op1=ALU.add), reads=[ps.b, XT.b[oc]], writes=[XT.b[oc]])
                                k.op("pool", lambda: Pq.tensor_tensor(out=sq.t[:, oc, 0:N], in0=XT.t[:, oc, 0:N], in1=XT.t[:, oc, 0:N], op=ALU.mult), reads=[XT.b[oc]], writes=[sq.b[oc]])
                                if oc > 0:
                                    ss_mm(oc - 1)
                            ss_mm(7)
                            k.op("act", lambda: A.activation(out=lnv.t[:, 0:N], in_=ssp.t[:, 0:N], func=AF.Ln, scale=1.0 / D, bias=EPS), reads=[ssp.b], writes=lnv.b)
                            k.op("act", lambda: A.activation(out=rstd.t[:, 0:N], in_=lnv.t[:, 0:N], func=AF.Exp, scale=-0.5), reads=lnv.b, writes=rstd.b)
                            k.dma("pool", x1s[b].rearrange("c p n -> p c n")[:, :, T0:T0 + N], XT.t[:, :, 0:N], reads=XT.b, writes=[x1b[b][ti]])
                            k.dma("pool", rs2[b][:, T0:T0 + N], rstd.t[:, 0:N], reads=rstd.b, writes=[rsb[b][ti]])
                        k.barrier()
                    if stop_after == "BC":
                        return nc
            with ExitStack() as ph:
                wgu = sb("wgu", [128, 8, 2 * DFF], BF16, n=12, stack=ph)
                wdn = sb("wdn", [128, 22, 1024], BF16, n=6, stack=ph)
                gsrc = d["w_gu"][l].rearrange("c p n -> p c n")
                dsrc = d["w_dn"][l].rearrange("c p n -> p c n")
                for g in range(6):
                    c0, c1 = g * 512, min(DFF, (g + 1) * 512)
                    k.dma("pool", wgu.t[:, :, c0:c1], gsrc[:, :, c0:c1], writes=[wgu.b[2 * g]])
                    k.dma("pool", wgu.t[:, :, DFF + c0:DFF + c1], gsrc[:, :, DFF + c0:DFF + c1], writes=[wgu.b[2 * g + 1]])
                for g in range(6):
                    r0, r1 = g * 4, min(22, (g + 1) * 4)
                    k.dma("pool", wdn.t[:, r0:r1, :], dsrc[:, r0:r1, :], writes=[wdn.b[g]])
                XR = Rot([sb(f"XF{i}", [128, 8, FT], F32, n=8, stack=ph) for i in range(2)])
                RR = Rot([sb(f"RF{i}", [128, FT], F32, stack=ph) for i in range(2)])
                fT = sb("fT", [128, 8, FT], BF16, n=8, stack=ph)
                aT = sb("aT", [128, 22, FT], BF16, n=22, stack=ph)
                xnR = Rot([sb(f"xn{i}", [128, FT], F32, stack=ph) for i in range(2)])
                sgR = Rot([sb(f"sg{i}", [128, FT], F32, stack=ph) for i in range(2)])
                items = [(b, T0, N, isctx) for b in range(NB) for (T0, N, isctx) in FTILES if not (isctx and l == 1)]
                XRs = {}

                def f_load(i):
                    b, T0, N, isctx = items[i]
                    pti = 0 if isctx else 1 + (T0 - CTX) // 512
                    X = XR.next()
                    R = RR.next()
                    XRs[i] = (X, R)
                    k.dma("sp", X.t[:, :, 0:N], x1s[b].rearrange("c p n -> p c n")[:, :, T0:T0 + N], reads=[x1b[b][pti]], writes=X.b)
                    k.dma("sp", R.t[:, 0:N], rs2[b][:, T0:T0 + N], reads=[rsb[b][pti]], writes=R.b)

                def f_norm(i):
                    b, T0, N, isctx = items[i]
                    col = 2 if isctx else b
                    X, R = XRs[i]
                    for c in range(8):
                        xn = xnR.next()
                        k.op("pool", lambda: Pq.tensor_tensor(out=xn.t[:, 0:N], in0=X.t[:, c, 0:N], in1=R.t[:, 0:N], op=ALU.mult), reads=[X.b[c]] + R.b, writes=xn.b)
                        k.op("dve", lambda: V.tensor_scalar(out=fT.t[:, c, 0:N], in0=xn.t[:, 0:N], scalar1=gsc2.t[:, l, c, col:col + 1],
                                                            scalar2=modT.t[:, l, 24 + c, col:col + 1], op0=ALU.mult, op1=ALU.add), reads=xn.b, writes=[fT.b[c]])

                def f_gateup(i):
                    b, T0, N, isctx = items[i]
                    for j in range(22):
                        gps = PSUM.next()
                        ups = PSUM.next()
                        for dc in range(8):
                            k.op("pe", lambda: nc.tensor.matmul(gps.t[:, 0:N], lhsT=wgu.t[:, dc, j * 128:(j + 1) * 128], rhs=fT.t[:, dc, 0:N], start=(dc == 0), stop=(dc == 7)),
                                 reads=[fT.b[dc], wgu.b[2 * (j // 4)]], writes=[gps.b])
                        for dc in range(8):
                            k.op("pe", lambda: nc.tensor.matmul(ups.t[:, 0:N], lhsT=wgu.t[:, dc, DFF + j * 128:DFF + (j + 1) * 128], rhs=fT.t[:, dc, 0:N], start=(dc == 0), stop=(dc == 7)),
                                 reads=[fT.b[dc], wgu.b[2 * (j // 4) + 1]], writes=[ups.b])
                        sg = sgR.next()
                        k.op("act", lambda: A.activation(out=sg.t[:, 0:N], in_=gps.t[:, 0:N], func=AF.Silu), reads=[gps.b], writes=sg.b)
                        k.op("dve", lambda: V.tensor_tensor(out=aT.t[:, j, 0:N], in0=ups.t[:, 0:N], in1=sg.t[:, 0:N], op=ALU.mult), reads=[ups.b] + sg.b, writes=[aT.b[j]])

                def f_down(i):
                    b, T0, N, isctx = items[i]
                    col = 2 if isctx else b
                    pti = 0 if isctx else 1 + (T0 - CTX) // 512
                    X, R = XRs[i]
                    for oc in range(8):
                        dps = PSUM.next()
                        for j in range(22):
                            k.op("pe", lambda: nc.tensor.matmul(dps.t[:, 0:N], lhsT=wdn.t[:, j, oc * 128:(oc + 1) * 128], rhs=aT.t[:, j, 0:N], start=(j == 0), stop=(j == 21)),
                                 reads=[aT.b[j], wdn.b[j // 4]], writes=[dps.b])
                        k.op("dve", lambda: V.scalar_tensor_tensor(out=X.t[:, oc, 0:N], in0=dps.t[:, 0:N], scalar=modT.t[:, l, 40 + oc, col:col + 1], in1=X.t[:, oc, 0:N],
                                                                   op0=ALU.mult, op1=ALU.add), reads=[dps.b, X.b[oc]], writes=[X.b[oc]])
                    if l == 0:
                        k.dma("pool", x2s[b].rearrange("c p n -> p c n")[:, :, T0:T0 + N], X.t[:, :, 0:N], reads=X.b, writes=[x2b[b][pti]])
                    else:
                        k.dma("pool", yT[b].rearrange("c p n -> p c n")[:, :, T0 - CTX:T0 - CTX + N], X.t[:, :, 0:N], reads=X.b)

                f_load(0)
                f_norm(0)
                for i in range(len(items)):
                    if i + 1 < len(items):
                        f_load(i + 1)
                    f_gateup(i)
                    if i + 1 < len(items):
                        f_norm(i + 1)
                    f_down(i)
                k.barrier()
            if stop_after == "F":
                return nc
        k.barrier()
    return nc


_CONST = {}


def _constants():
    if _CONST:
        return _CONST
    n_freq = 16
    freqs = (10000.0 ** (-np.arange(n_freq, dtype=np.float32) / n_freq)).astype(np.float32)
    t = np.arange(T)
    pos = [(t // 64).astype(np.float32), (t % 64).astype(np.float32)]
    cos = np.zeros((128, T), np.float32)
    sin = np.zeros((128, T), np.float32)
    prot = np.zeros((128, 128), np.float32)
    for p in range(128):
        dd = p % 64
        s, half, f = dd // 32, (dd % 32) // 16, dd % 16
        ang = (pos[s] * freqs[f]).astype(np.float32)
        cos[p] = np.cos(ang)
        sin[p] = np.sin(ang) * (-1.0 if half == 0 else 1.0)
        partner = p + 16 if half == 0 else p - 16
        prot[partner, p] = 1.0
    _CONST["rope_cos"] = cos
    _CONST["rope_sin"] = sin
    _CONST["prot"] = prot.astype(NPBF)
    c64 = np.arange(64)
    ang = 2 * np.pi * np.outer(c64, c64) / 64.0
    cc, sc = np.cos(ang), np.sin(ang)
    z = np.zeros((64, 64))
    _CONST["ccbd"] = np.block([[cc, z], [z, cc]]).astype(np.float32)
    _CONST["scbd"] = np.block([[sc, z], [z, sc]]).astype(np.float32)
    li = np.arange(T, dtype=np.int64)
    m = (np.outer(li, li) % T).astype(np.float64) * (2 * np.pi / T)
    _CONST["dft_c"] = (np.cos(m) / 512.0).astype(np.float32).astype(NPBF).reshape(32, 128, T)
    _CONST["dft_s"] = (-np.sin(m) / 512.0).astype(np.float32).astype(NPBF).reshape(32, 128, T)
    l2 = np.arange(CTX, dtype=np.int64)
    m2 = (np.outer(l2, l2) % CTX).astype(np.float64) * (2 * np.pi / CTX)
    c2 = (np.cos(m2) / 128.0).reshape(2, 128, CTX)
    s2 = (-np.sin(m2) / 128.0).reshape(2, 128, CTX)
    _CONST["dft256"] = np.stack([c2, s2], 0).transpose(2, 0, 1, 3).astype(np.float32).astype(NPBF)
    Ls = 384
    bands = np.zeros((128, 4, 5, 128), np.float32)
    tt = np.arange(Ls)
    for g, w in enumerate(POOLW):
        lo = np.clip(tt - w // 2, 0, Ls)
        hi = np.clip(tt + w - w // 2, 0, Ls)
        Dm = np.zeros((Ls, Ls), np.float64)
        for tp in range(Ls):
            Dm[lo[tp]:hi[tp], tp] = 1.0 / (hi[tp] - lo[tp])
            Dm[tp, tp] -= 1.0
        bands[:, g, 0] = Dm[0:128, 128:256]
        bands[:, g, 1] = Dm[256:384, 128:256]
        bands[:, g, 2] = Dm[0:128, 0:128]
        bands[:, g, 3] = Dm[128:256, 128:256]
        bands[:, g, 4] = Dm[256:384, 256:384]
    _CONST["bands"] = bands.reshape(128, 2560).astype(NPBF)
    return _CONST


def _bd(a, b):
    z = np.zeros((64, 64), np.float32)
    return np.block([[a, z], [z, b]])


def _prep_shared(inp):
    f = lambda a: np.ascontiguousarray(np.asarray(a, dtype=np.float32))
    sh = dict(_constants())
    sh["w_ada"] = f(inp["w_ada"]).reshape(2, 8, 128, 6144)
    sh["b_ada"] = f(f(inp["b_ada"]).reshape(2, 48, 128).transpose(0, 2, 1))
    sh["g_mix"] = f(f(inp["g_mix"]).reshape(2, 8, 128).transpose(0, 2, 1))
    sh["g_ffn"] = f(f(inp["g_ffn"]).reshape(2, 8, 128).transpose(0, 2, 1))
    w_in = f(inp["w_in"])
    qcols = []
    for c in range(4):
        qcols += list(range(64 * c, 64 * c + 64)) + list(range(64 * (c + 4), 64 * (c + 4) + 64))
    cols = qcols + list(range(512, 640)) + list(range(1024, 1280)) + list(range(640, 768)) + list(range(768, 1024))
    sh["w_in"] = f(w_in[:, :, cols]).reshape(2, 8, 128, 1280)
    qg, kg = f(inp["q_gain"]), f(inp["k_gain"])
    sh["qk_gain"] = f(np.stack([np.tile(qg, (1, 2)), np.tile(kg, (1, 2))], axis=-1))
    wp, wf = f(inp["w_pool"]), f(inp["w_four"])
    sh["w_pool"] = f(np.stack([np.stack([_bd(wp[l, 2 * pc], wp[l, 2 * pc + 1]) for pc in range(2)], 1) for l in range(2)], 0))
    sh["w_four"] = f(np.stack([np.stack([_bd(wf[l, 2 * pc], wf[l, 2 * pc + 1]) for pc in range(2)], 1) for l in range(2)], 0))
    sh["pool_scale"] = f(f(inp["pool_scale"]).reshape(2, 2, 128).transpose(0, 2, 1))
    w_out = f(inp["w_out"])
    rows = qcols + list(range(512, 1024))
    sh["w_out"] = f(w_out[:, rows, :]).reshape(2, 8, 128, 1024)
    sh["w_gu"] = f(inp["w_gate_up"]).reshape(2, 8, 128, 2 * DFF)
    sh["w_dn"] = f(inp["w_down"]).reshape(2, 22, 128, 1024)
    return sh


def _prep_core(inp, i):
    f = lambda a: np.ascontiguousarray(np.asarray(a, dtype=np.float32))
    b0 = NB * i
    x = np.asarray(inp["x"])[b0:b0 + NB]
    ctx = np.asarray(inp["ctx"])[b0:b0 + NB]
    c = np.asarray(inp["c"])[b0:b0 + NB]
    cctx = np.asarray(inp["c_ctx"])
    m = {}
    m["xT"] = f(x.transpose(0, 2, 1)).reshape(NB, 8, 128, T)
    m["ctxT"] = f(ctx.transpose(0, 2, 1)).reshape(NB, 8, 128, CTX)
    cc = np.stack([c[0], c[1], cctx, cctx], axis=1)
    m["cc"] = f(cc.reshape(8, 128, 4).transpose(1, 0, 2))
    return m


_NC_CACHE = {}


def kernel(**inputs):
    if "nc" not in _NC_CACHE:
        _NC_CACHE["nc"] = build_program()
    nc = _NC_CACHE["nc"]
    sh = _prep_shared(inputs)
    in_maps = []
    for i in range(NCORES):
        m = dict(sh)
        m.update(_prep_core(inputs, i))
        in_maps.append(m)
    res = run_bass_kernel_spmd(nc, in_maps, core_ids=list(range(NCORES)))
    outs = []
    for i in range(NCORES):
        y = np.asarray(res.results[i]["yT"]).reshape(NB, D, T)
        outs.append(y.transpose(0, 2, 1))
    return np.ascontiguousarray(np.concatenate(outs, axis=0).astype(np.float32))
```

```python
import numpy as np
import ml_dtypes
from contextlib import ExitStack
import concourse.bass as bass
import concourse.mybir as mybir
from concourse.bass_utils import run_bass_kernel_spmd

F32 = mybir.dt.float32
BF16 = mybir.dt.bfloat16
AF = mybir.ActivationFunctionType
ALU = mybir.AluOpType
NPBF = ml_dtypes.bfloat16

NCORES = 8
NB = 2
T = 4096
CTX = 256
TT = T + CTX
D = 1024
DFF = 2816
EPS = 1e-6
POOLW = (2, 4, 8, 16)
TILES = [(0, 256, True)] + [(256 + 512 * i, 512, False) for i in range(8)]
FT = 256
FTILES = [(0, 256, True)] + [(256 + FT * i, FT, False) for i in range(T // FT)]
NST = TT // 128


class Buf:
    __slots__ = ("w", "r")

    def __init__(self):
        self.w = None
        self.r = {}


class TB:
    def __init__(self, t, n=1):
        self.t = t
        self.b = [Buf() for _ in range(n)]


class Rot:
    def __init__(self, items):
        self.items = items
        self.i = 0

    def next(self):
        it = self.items[self.i]
        self.i = (self.i + 1) % len(self.items)
        return it


NDS = 8


class KB:
    def __init__(self, nc, es):
        self.nc = nc
        self.E = dict(pe=nc.tensor, act=nc.scalar, dve=nc.vector, pool=nc.gpsimd, sp=nc.sync)
        self.csem = {e: es.enter_context(nc.semaphore("c_" + e)) for e in ["pe", "act", "dve", "pool"]}
        self.cnt = {e: 0 for e in self.csem}
        self.waited = {e: {} for e in self.E}
        self.dsem = {}
        for q in ["sp", "pool"]:
            self.dsem[q] = [[es.enter_context(nc.semaphore(f"d_{q}{i}")), 0] for i in range(NDS)]
        self.drr = {q: 0 for q in self.dsem}
        self.nwait = 0

    def _wait(self, eng, h):
        key, sem, val = h
        if eng == "pe" and key == "c_pe":
            return
        w = self.waited[eng]
        if w.get(key, 0) >= val:
            return
        self.E[eng].wait_ge(sem, val)
        self.nwait += 1
        w[key] = val

    def _deps(self, eng, reads, writes):
        for b in reads:
            if b.w is not None:
                self._wait(eng, b.w)
        for b in writes:
            if b.w is not None:
                self._wait(eng, b.w)
            for h in b.r.values():
                self._wait(eng, h)

    @staticmethod
    def _mark(h, reads, writes):
        for b in reads:
            old = b.r.get(h[0])
            if old is None or old[2] < h[2]:
                b.r[h[0]] = h
        for b in writes:
            b.w = h
            b.r = {}

    def op(self, eng, ins_fn, reads=(), writes=()):
        self._deps(eng, reads, writes)
        ins = ins_fn()
        self.cnt[eng] += 1
        ins.then_inc(self.csem[eng], 1)
        h = ("c_" + eng, self.csem[eng], self.cnt[eng])
        self._mark(h, reads, writes)
        return h

    def dma(self, q, out, in_, reads=(), writes=()):
        self._deps(q, reads, writes)
        idx = self.drr[q]
        self.drr[q] = (idx + 1) % NDS
        slot = self.dsem[q][idx]
        sem, c = slot
        key = f"d_{q}{idx}"
        if c > 0:
            self._wait(q, (key, sem, 16 * c))
        ins = self.E[q].dma_start(out=out, in_=in_)
        ins.then_inc(sem, 16)
        slot[1] = c + 1
        h = (key, sem, 16 * (c + 1))
        self._mark(h, reads, writes)
        return h

    def barrier(self):
        hs = [("c_" + e, self.csem[e], self.cnt[e]) for e in self.csem if self.cnt[e] > 0]
        for q, slots in self.dsem.items():
            for i, (sem, c) in enumerate(slots):
                if c > 0:
                    hs.append((f"d_{q}{i}", sem, 16 * c))
        for eng in self.E:
            for h in hs:
                key, sem, val = h
                w = self.waited[eng]
                if w.get(key, 0) >= val:
                    continue
                self.E[eng].wait_ge(sem, val)
                w[key] = val


def build_program(debug=False, stop_after=None):
    nc = bass.Bass("TRN2", target_bir_lowering=False)
    d = {}

    def din(name, shape, dt):
        d[name] = nc.dram_tensor(name, shape, dt, kind="ExternalInput").ap()

    din("xT", [NB, 8, 128, T], F32)
    din("ctxT", [NB, 8, 128, CTX], F32)
    din("cc", [128, 8, 4], F32)
    din("w_ada", [2, 8, 128, 6144], F32)
    din("b_ada", [2, 128, 48], F32)
    din("g_mix", [2, 128, 8], F32)
    din("g_ffn", [2, 128, 8], F32)
    din("w_in", [2, 8, 128, 1280], F32)
    din("qk_gain", [2, 128, 2], F32)
    din("w_pool", [2, 128, 2, 128], F32)
    din("pool_scale", [2, 128, 2], F32)
    din("w_four", [2, 128, 2, 128], F32)
    din("w_out", [2, 8, 128, 1024], F32)
    din("w_gu", [2, 8, 128, 2 * DFF], F32)
    din("w_dn", [2, 22, 128, 1024], F32)
    din("rope_cos", [128, T], F32)
    din("rope_sin", [128, T], F32)
    din("prot", [128, 128], BF16)
    din("ccbd", [128, 128], F32)
    din("scbd", [128, 128], F32)
    din("dft_c", [32, 128, T], BF16)
    din("dft_s", [32, 128, T], BF16)
    din("dft256", [128, 2, 2, 256], BF16)
    din("bands", [128, 4 * 5 * 128], BF16)
    skind = "ExternalOutput" if debug else "Internal"
    yT = nc.dram_tensor("yT", [NB, 8, 128, T], F32, kind="ExternalOutput").ap()
    x1s = nc.dram_tensor("x1s", [NB, 8, 128, TT], F32, kind=skind).ap()
    x2s = nc.dram_tensor("x2s", [NB, 8, 128, TT], F32, kind=skind).ap()
    rs2 = nc.dram_tensor("rs2", [NB, 128, TT], F32, kind=skind).ap()
    dbg = {}
    if debug:
        dbg["qt"] = nc.dram_tensor("dbg_qt", [128, 4, TT], BF16, kind="ExternalOutput").ap()
        dbg["kt"] = nc.dram_tensor("dbg_kt", [128, TT], BF16, kind="ExternalOutput").ap()
        dbg["vp"] = nc.dram_tensor("dbg_vp", [128, NST, 448], BF16, kind="ExternalOutput").ap()
        dbg["uab"] = nc.dram_tensor("dbg_uab", [128, NST, 512], BF16, kind="ExternalOutput").ap()
        dbg["mod"] = nc.dram_tensor("dbg_mod", [128, 2, 48, 4], F32, kind="ExternalOutput").ap()
        dbg["mx"] = nc.dram_tensor("dbg_mx", [128, 8, TT], BF16, kind="ExternalOutput").ap()

    x1b = [[Buf() for _ in TILES] for _ in range(NB)]
    x2b = [[Buf() for _ in TILES] for _ in range(NB)]
    rsb = [[Buf() for _ in TILES] for _ in range(NB)]
    CONSTB = Buf()

    with ExitStack() as es:
        k = KB(nc, es)

        uid = [0]

        def sb(name, shape, dt, n=1, stack=None):
            uid[0] += 1
            t = (stack or es).enter_context(nc.sbuf_tensor(f"s{uid[0]}_{name}", shape, dt))
            return TB(t, n)

        psum_all = es.enter_context(nc.psum_tensor("psum_all", [128, 4096], F32))
        PSB = [Buf() for _ in range(8)]

        class PS:
            def __init__(self, i):
                self.i = i
                self.t = psum_all[:, i * 512:(i + 1) * 512]
                self.b = PSB[i]

        PSUM = Rot([PS(i) for i in range(8)])

        ones = sb("ones", [128, 128], BF16)
        bones = sb("bones", [128, 128], BF16)
        prot = sb("prot", [128, 128], BF16)
        bands = sb("bands", [128, 2560], BF16)
        d256 = sb("d256", [128, 2, 2, 256], BF16)
        ccbd = sb("ccbd", [128, 128], F32)
        scbd = sb("scbd", [128, 128], F32)
        scT = sb("scT", [128, 8, 4], F32)
        modT = sb("modT", [128, 2, 48, 4], F32)
        bada = sb("bada", [128, 2, 48], F32)
        gmix = sb("gmix", [128, 2, 8], F32)
        gffn = sb("gffn", [128, 2, 8], F32)
        qkg = sb("qkg", [128, 2, 2], F32)
        pscale = sb("pscale", [128, 2, 2], F32)
        wpb = sb("wpb", [128, 2, 2, 128], BF16)
        wf32 = sb("wf32", [128, 2, 2, 128], F32)
        abd = sb("abd", [128, 2, 4, 128], BF16)
        gsc1 = sb("gsc1", [128, 2, 8, 4], F32)
        gsc2 = sb("gsc2", [128, 2, 8, 4], F32)
        tmpm = sb("tmpm", [128, 8, 4], F32)

        V, Pq, A, S_ = nc.vector, nc.gpsimd, nc.scalar, nc.sync
        k.op("dve", lambda: V.memset(ones.t[:], 1.0), writes=ones.b)
        k.op("dve", lambda: V.memset(bones.t[:], 0.0), writes=bones.b)
        k.op("dve", lambda: V.memset(bones.t[0:64, 0:64], 1.0), writes=bones.b)
        k.op("dve", lambda: V.memset(bones.t[64:128, 64:128], 1.0), writes=bones.b)
        k.dma("sp", prot.t[:], d["prot"][:, :], writes=prot.b)
        k.dma("sp", bands.t[:], d["bands"][:, :], writes=bands.b)
        k.dma("sp", d256.t[:], d["dft256"][:, :, :, :], writes=d256.b)
        k.dma("sp", ccbd.t[:], d["ccbd"][:, :], writes=ccbd.b)
        k.dma("sp", scbd.t[:], d["scbd"][:, :], writes=scbd.b)
        k.dma("sp", scT.t[:], d["cc"][:, :, :], writes=scT.b)
        for l in range(2):
            k.dma("sp", bada.t[:, l, :], d["b_ada"][l], writes=bada.b)
            k.dma("sp", gmix.t[:, l, :], d["g_mix"][l], writes=gmix.b)
            k.dma("sp", gffn.t[:, l, :], d["g_ffn"][l], writes=gffn.b)
            k.dma("sp", qkg.t[:, l, :], d["qk_gain"][l], writes=qkg.b)
            k.dma("sp", pscale.t[:, l, :], d["pool_scale"][l], writes=pscale.b)
            k.dma("sp", wf32.t[:, l, :, :], d["w_four"][l], writes=wf32.b)
            k.dma("pool", wpb.t[:, l, :, :], d["w_pool"][l], writes=wpb.b)
        k.op("act", lambda: A.activation(out=scT.t[:], in_=scT.t[:], func=AF.Silu), reads=scT.b, writes=scT.b)
        for l in range(2):
            for ab in range(2):
                for fc in range(2):
                    ps = PSUM.next()
                    cs = ccbd if ab == 0 else scbd
                    k.op("pe", lambda: nc.tensor.matmul(ps.t[:, 0:128], lhsT=cs.t[:], rhs=wf32.t[:, l, fc, :], start=True, stop=True),
                         reads=cs.b + wf32.b, writes=[ps.b])
                    k.op("dve", lambda: V.tensor_copy(out=abd.t[:, l, ab * 2 + fc, :], in_=ps.t[:, 0:128]), reads=[ps.b], writes=abd.b)

        with ExitStack() as ph:
            wa = Rot([sb(f"wa{i}", [128, 8, 512], F32, stack=ph) for i in range(2)])
            for l in range(2):
                for jb in range(12):
                    w = wa.next()
                    k.dma("sp", w.t[:], d["w_ada"][l].rearrange("c p n -> p c n")[:, :, jb * 512:(jb + 1) * 512], writes=w.b)
                    for jj in range(4):
                        j = jb * 4 + jj
                        ps = PSUM.next()
                        for dc in range(8):
                            k.op("pe", lambda: nc.tensor.matmul(ps.t[:, 0:4], lhsT=w.t[:, dc, jj * 128:(jj + 1) * 128], rhs=scT.t[:, dc, :],
                                                                start=(dc == 0), stop=(dc == 7)), reads=w.b + scT.b, writes=[ps.b])
                        k.op("dve", lambda: V.tensor_scalar(out=modT.t[:, l, j, :], in0=ps.t[:, 0:4], scalar1=bada.t[:, l, j:j + 1], scalar2=None,
                                                            op0=ALU.add), reads=[ps.b] + bada.b, writes=modT.b)
                for (gs, g, base) in ((gsc1, gmix, 8), (gsc2, gffn, 32)):
                    k.op("dve", lambda: V.tensor_scalar(out=tmpm.t[:], in0=modT.t[:, l, base:base + 8, :], scalar1=1.0, scalar2=None, op0=ALU.add),
                         reads=modT.b, writes=tmpm.b)
                    for col in range(3):
                        k.op("dve", lambda: V.tensor_tensor(out=gs.t[:, l, :, col], in0=tmpm.t[:, :, col], in1=g.t[:, l, :], op=ALU.mult),
                             reads=tmpm.b + g.b, writes=gs.b)
            if debug:
                k.dma("sp", dbg["mod"][:, :, :, :], modT.t[:], reads=modT.b)
            k.barrier()

        def xsrc(l, b, ti):
            T0, N, isctx = TILES[ti]
            if l == 0:
                if isctx:
                    return d["ctxT"][b].rearrange("c p n -> p c n"), None
                return d["xT"][b].rearrange("c p n -> p c n")[:, :, T0 - CTX:T0 - CTX + N], None
            return x2s[b].rearrange("c p n -> p c n")[:, :, T0:T0 + N], x2b[b][ti]

        def load_w(wt, src, ncol, rows):
            for c0 in range(0, ncol, 1024):
                c1 = min(ncol, c0 + 1024)
                for r0 in range(0, rows, 8):
                    r1 = min(rows, r0 + 8)
                    k.dma("pool", wt.t[:, r0:r1, c0:c1], src.rearrange("c p n -> p c n")[:, r0:r1, c0:c1], writes=wt.b)

        for l in range(2):
            for b in range(NB):
                with ExitStack() as mix:
                    QT = sb("QT", [128, 4, TT], BF16, stack=mix)
                    QTb = [[Buf() for _ in TILES] for _ in range(4)]
                    KT = sb("KT", [128, TT], BF16, n=len(TILES), stack=mix)
                    VP = sb("VP", [128, NST, 448], BF16, n=len(TILES), stack=mix)
                    k.op("dve", lambda: V.memset(VP.t[:, :, 64:128], 1.0), writes=VP.b)
                    UAB = sb("UAB", [128, NST, 512], BF16, n=len(TILES), stack=mix)
                    XT = sb("XT", [128, 8, 512], F32, n=8, stack=mix)
                    sq = sb("sq", [128, 8, 512], BF16, n=8, stack=mix)
                    lnv = sb("lnv", [128, 512], F32, stack=mix)
                    rstd = sb("rstd", [128, 512], F32, stack=mix)
                    with ExitStack() as ph:
                        win = sb("win", [128, 8, 1280], BF16, stack=ph)
                        load_w(win, d["w_in"][l], 1280, 8)
                        hTs = [sb(f"hT{i}", [128, 8, 512], BF16, n=8, stack=ph) for i in range(2)]
                        cs = sb("cs", [128, 2, 512], F32, stack=ph)
                        zsqR = Rot([sb(f"zsq{i}", [128, 512], BF16, stack=ph) for i in range(2)])
                        hrR = Rot([sb(f"hr{i}", [128, 512], F32, stack=ph) for i in range(2)])
                        zgR = Rot([sb(f"zg{i}", [128, 512], BF16, stack=ph) for i in range(2)])
                        t1R = Rot([sb(f"t1{i}", [128, 512], F32, stack=ph) for i in range(2)])
                        t2R = Rot([sb(f"t2{i}", [128, 512], F32, stack=ph) for i in range(1)])
                        uT = [sb(f"uT{i}", [128, 512], BF16, stack=ph) for i in range(2)]

                        def normA(ti):
                            T0, N, isctx = TILES[ti]
                            col = 2 if isctx else b
                            hT = hTs[ti % 2]
                            src, srcb = xsrc(l, b, ti)
                            k.dma("sp", XT.t[:, :, 0:N], src, reads=[srcb] if srcb else [], writes=XT.b)
                            k.op("act", lambda: A.activation(out=sq.t[:, :, 0:N], in_=XT.t[:, :, 0:N], func=AF.Square), reads=XT.b, writes=sq.b)
                            ps = PSUM.next()
                            for c in range(8):
                                k.op("pe", lambda: nc.tensor.matmul(ps.t[:, 0:N], lhsT=ones.t[:], rhs=sq.t[:, c, 0:N], start=(c == 0), stop=(c == 7)),
                                     reads=sq.b, writes=[ps.b])
                            k.op("act", lambda: A.activation(out=lnv.t[:, 0:N], in_=ps.t[:, 0:N], func=AF.Ln, scale=1.0 / D, bias=EPS), reads=[ps.b], writes=lnv.b)
                            k.op("act", lambda: A.activation(out=rstd.t[:, 0:N], in_=lnv.t[:, 0:N], func=AF.Exp, scale=-0.5), reads=lnv.b, writes=rstd.b)
                            for c in range(8):
                                if c % 2 == 0:
                                    k.op("pool", lambda: Pq.tensor_tensor(out=XT.t[:, c, 0:N], in0=XT.t[:, c, 0:N], in1=rstd.t[:, 0:N], op=ALU.mult),
                                         reads=[XT.b[c]] + rstd.b, writes=[XT.b[c]])
                                else:
                                    k.op("dve", lambda: V.tensor_tensor(out=XT.t[:, c, 0:N], in0=XT.t[:, c, 0:N], in1=rstd.t[:, 0:N], op=ALU.mult),
                                         reads=[XT.b[c]] + rstd.b, writes=[XT.b[c]])
                                k.op("act", lambda: A.activation(out=hT.t[:, c, 0:N], in_=XT.t[:, c, 0:N], func=AF.Identity, scale=gsc1.t[:, l, c, col:col + 1],
                                                                 bias=modT.t[:, l, c, col:col + 1]), reads=[XT.b[c]], writes=[hT.b[c]])

                        normA(0)
                        for ti, (T0, N, isctx) in enumerate(TILES):
                            col = 2 if isctx else b
                            full = not (isctx and l == 1)
                            st0 = T0 // 128
                            hT = hTs[ti % 2]
                            if not isctx:
                                k.dma("sp", cs.t[:, 0, 0:N], d["rope_cos"][:, T0 - CTX:T0 - CTX + N], writes=cs.b)
                                k.dma("sp", cs.t[:, 1, 0:N], d["rope_sin"][:, T0 - CTX:T0 - CTX + N], writes=cs.b)
                            ocs = list(range(7)) if full else [4]
                            stt = {}

                            def stage0(oc):
                                ps = PSUM.next()
                                for dc in range(8):
                                    k.op("pe", lambda: nc.tensor.matmul(ps.t[:, 0:N], lhsT=win.t[:, dc, oc * 128:(oc + 1) * 128], rhs=hT.t[:, dc, 0:N],
                                                                        start=(dc == 0), stop=(dc == 7)), reads=[hT.b[dc]] + win.b, writes=[ps.b])
                                stt[oc] = {"ps": ps}

                            def stage1(oc):
                                ps = stt[oc]["ps"]
                                if oc >= 5:
                                    u = uT[oc - 5]
                                    k.op("dve", lambda: V.tensor_copy(out=u.t[:, 0:N], in_=ps.t[:, 0:N]), reads=[ps.b], writes=u.b)
                                    return
                                zsq = zsqR.next()
                                k.op("act", lambda: A.activation(out=zsq.t[:, 0:N], in_=ps.t[:, 0:N], func=AF.Square), reads=[ps.b], writes=zsq.b)
                                hs = PSUM.next()
                                k.op("pe", lambda: nc.tensor.matmul(hs.t[:, 0:N], lhsT=bones.t[:], rhs=zsq.t[:, 0:N], start=True, stop=True),
                                     reads=zsq.b, writes=[hs.b])
                                hr = hrR.next()
                                k.op("act", lambda: A.activation(out=hr.t[:, 0:N], in_=hs.t[:, 0:N], func=AF.Ln, scale=1.0 / 64, bias=EPS), reads=[hs.b], writes=hr.b)
                                k.op("act", lambda: A.activation(out=hr.t[:, 0:N], in_=hr.t[:, 0:N], func=AF.Exp, scale=-0.5), reads=hr.b, writes=hr.b)
                                gcol = 0 if oc < 4 else 1
                                if oc < 4:
                                    dest, destb = QT.t[:, oc, T0:T0 + N], [QTb[oc][ti]]
                                else:
                                    dest, destb = KT.t[:, T0:T0 + N], [KT.b[ti]]
                                stt[oc]["dest"] = (dest, destb)
                                if isctx:
                                    k.op("dve", lambda: V.scalar_tensor_tensor(out=dest, in0=ps.t[:, 0:N], scalar=qkg.t[:, l, gcol:gcol + 1], in1=hr.t[:, 0:N],
                                                                               op0=ALU.mult, op1=ALU.mult), reads=[ps.b] + hr.b, writes=destb)
                                else:
                                    zg = zgR.next()
                                    stt[oc]["zg"] = zg
                                    k.op("dve", lambda: V.scalar_tensor_tensor(out=zg.t[:, 0:N], in0=ps.t[:, 0:N], scalar=qkg.t[:, l, gcol:gcol + 1], in1=hr.t[:, 0:N],
                                                                               op0=ALU.mult, op1=ALU.mult), reads=[ps.b] + hr.b, writes=zg.b)

                            def stage2(oc):
                                if oc >= 5 or isctx:
                                    return
                                zg = stt[oc]["zg"]
                                dest, destb = stt[oc]["dest"]
                                zr = PSUM.next()
                                k.op("pe", lambda: nc.tensor.matmul(zr.t[:, 0:N], lhsT=prot.t[:], rhs=zg.t[:, 0:N], start=True, stop=True), reads=zg.b, writes=[zr.b])
                                t1 = t1R.next()
                                t2 = t2R.next()
                                k.op("pool", lambda: Pq.tensor_tensor(out=t1.t[:, 0:N], in0=zg.t[:, 0:N], in1=cs.t[:, 0, 0:N], op=ALU.mult), reads=zg.b + cs.b, writes=t1.b)
                                k.op("dve", lambda: V.tensor_tensor(out=t2.t[:, 0:N], in0=zr.t[:, 0:N], in1=cs.t[:, 1, 0:N], op=ALU.mult), reads=[zr.b] + cs.b, writes=t2.b)
                                k.op("pool", lambda: Pq.tensor_tensor(out=dest, in0=t1.t[:, 0:N], in1=t2.t[:, 0:N], op=ALU.add), reads=t1.b + t2.b, writes=destb)

                            n = len(ocs)
                            for s_ in range(n + 2):
                                if s_ < n:
                                    stage0(ocs[s_])
                                if 0 <= s_ - 1 < n:
                                    stage1(ocs[s_ - 1])
                                if 0 <= s_ - 2 < n:
                                    stage2(ocs[s_ - 2])
                            if ti + 1 < len(TILES):
                                normA(ti + 1)
                            for j in range(N // 128):
                                st = st0 + j
                                if full:
                                    ps2 = PSUM.next()
                                    for ab in range(2):
                                        for fc in range(2):
                                            q4 = ab * 2 + fc
                                            k.op("pe", lambda: nc.tensor.matmul(ps2.t[:, q4 * 128:(q4 + 1) * 128], lhsT=uT[fc].t[:, j * 128:(j + 1) * 128],
                                                                                rhs=abd.t[:, l, q4, :], start=True, stop=True), reads=uT[fc].b, writes=[ps2.b])
                                    k.op("dve", lambda: V.tensor_copy(out=UAB.t[:, st, :], in_=ps2.t[:, 0:512]), reads=[ps2.b], writes=[UAB.b[ti]])
                                ps3 = PSUM.next()
                                for dc in range(8):
                                    k.op("pe", lambda: nc.tensor.matmul(ps3.t[:, 0:384], lhsT=hT.t[:, dc, j * 128:(j + 1) * 128], rhs=win.t[:, dc, 896:1280],
                                                                        start=(dc == 0), stop=(dc == 7)), reads=[hT.b[dc]] + win.b, writes=[ps3.b])
                                k.op("act", lambda: A.activation(out=VP.t[:, st, 0:256].rearrange("p (a n) -> p a n", a=2)[:, :, 0:64],
                                                                 in_=ps3.t[:, 0:128].rearrange("p (a n) -> p a n", a=2), func=AF.Copy), reads=[ps3.b], writes=[VP.b[ti]])
                                k.op("dve", lambda: V.tensor_copy(out=VP.t[:, st, 192:448], in_=ps3.t[:, 128:384]), reads=[ps3.b], writes=[VP.b[ti]])
                        if debug and l == 0 and b == 0:
                            k.dma("sp", dbg["qt"][:, :, :], QT.t[:], reads=[x for r in QTb for x in r])
                            k.dma("sp", dbg["kt"][:, :], KT.t[:], reads=KT.b)
                            k.dma("sp", dbg["vp"][:, :, :], VP.t[:], reads=VP.b)
                            k.dma("sp", dbg["uab"][:, :, :], UAB.t[:], reads=UAB.b)
                        k.barrier()
                    if stop_after == "A":
                        k.barrier()
                        return nc
                    with ExitStack() as ph:
                        wout = sb("wout", [128, 8, 1024], BF16, stack=ph)
                        load_w(wout, d["w_out"][l], 1024, 8)
                        PTR = Rot([sb(f"PT{i}", [128, 1024], BF16, stack=ph) for i in range(3)])
                        DFR = Rot([sb(f"DF{i}", [128, 2, 4, 512], BF16, stack=ph) for i in range(2)])
                        MXP = sb("MXP", [128, 2, 512], BF16, n=2, stack=ph)
                        MXF = sb("MXF", [128, 2, 512], BF16, n=2, stack=ph)
                        LsR = Rot([sb(f"Ls{i}", [128, 512], F32, stack=ph) for i in range(2)])
                        OsR = Rot([sb(f"Os{i}", [128, 512], F32, stack=ph) for i in range(2)])
                        dTR = Rot([sb(f"dT{i}", [128, 512], BF16, stack=ph) for i in range(2)])
                        SR = Rot([(psum_all[:, 0:1024], [PSB[0], PSB[1]]), (psum_all[:, 1024:2048], [PSB[2], PSB[3]])])
                        PS4 = Rot([PS(i) for i in range(4, 8)])
                        for ti, (T0, N, isctx) in enumerate(TILES):
                            if isctx and l == 1:
                                continue
                            col = 2 if isctx else b
                            st0 = T0 // 128
                            src, srcb = xsrc(l, b, ti)

                            def load_xt():
                                k.dma("sp", XT.t[:, :, 0:N], src, reads=[srcb] if srcb else [], writes=XT.b)
                            kts = [0, 1] if isctx else list(range(NST))
                            steps = [(c, idx, kt) for c in range(4) for idx, kt in enumerate(kts)]

                            def emit_qk(c, kt):
                                kti = 0 if kt < 2 else 1 + (kt - 2) // 4
                                St, Sb = SR.next()
                                k.op("pe", lambda: nc.tensor.matmul(St[:, 0:N], lhsT=KT.t[0:64, kt * 128:(kt + 1) * 128], rhs=QT.t[0:64, c, T0:T0 + N],
                                                                    start=True, stop=True), reads=[KT.b[kti], QTb[c][ti]], writes=Sb)
                                k.op("pe", lambda: nc.tensor.matmul(St[:, 512:512 + N], lhsT=KT.t[64:128, kt * 128:(kt + 1) * 128], rhs=QT.t[64:128, c, T0:T0 + N],
                                                                    start=True, stop=True), reads=[KT.b[kti], QTb[c][ti]], writes=Sb)
                                return St, Sb
                            OL = {}
                            fq = []
                            yp = None
                            if not isctx:
                                lp0 = T0 - CTX
                                yp = [PS(6), PS(7)]
                                dfs = {}

                                def f_dma(ltg):
                                    df = DFR.next()
                                    dfs[ltg] = df
                                    k.dma("sp", df.t[:, 0, :, :], d["dft_c"][ltg * 4:(ltg + 1) * 4].rearrange("l p n -> p l n")[:, :, lp0:lp0 + 512], writes=df.b)
                                    k.dma("sp", df.t[:, 1, :, :], d["dft_s"][ltg * 4:(ltg + 1) * 4].rearrange("l p n -> p l n")[:, :, lp0:lp0 + 512], writes=df.b)

                                def f_mm(ltg, l4, csi):
                                    lt = ltg * 4 + l4
                                    lti = 1 + lt // 4
                                    df = dfs[ltg]
                                    for fc in range(2):
                                        k.op("pe", lambda: nc.tensor.matmul(yp[fc].t[:, 0:512], lhsT=UAB.t[:, 2 + lt, (csi * 2 + fc) * 128:(csi * 2 + fc + 1) * 128], rhs=df.t[:, csi, l4, :],
                                                                            start=(lt == 0 and csi == 0), stop=(lt == 31 and csi == 1)), reads=df.b + [UAB.b[lti]], writes=[yp[fc].b])
                                f_dma(0)
                                f_dma(1)
                                for ltg in range(8):
                                    for l4 in range(4):
                                        for csi in range(2):
                                            fq.append((f_mm, (ltg, l4, csi)))
                                    if ltg + 2 < 8:
                                        fq.append((f_dma, (ltg + 2,)))
                            load_xt()
                            fqi = 0
                            cumw = []
                            acc_w = 0
                            for (c_, idx_, kt_) in steps:
                                acc_w += 4 if (idx_ == 0 and c_ > 0) else 1
                                cumw.append(acc_w)
                            pend = emit_qk(steps[0][0], steps[0][2])
                            for si, (c, idx, kt) in enumerate(steps):
                                kti = 0 if kt < 2 else 1 + (kt - 2) // 4
                                St, Sb = pend
                                if idx == 0:
                                    OL[c] = (PS(4), PS(5))
                                O, Lp = OL[c]
                                PT = PTR.next()
                                if N == 512:
                                    k.op("act", lambda: A.activation(out=PT.t[:, :], in_=St[:, :], func=AF.Exp, scale=0.125), reads=Sb, writes=PT.b)
                                else:
                                    k.op("act", lambda: A.activation(out=PT.t[:, :].rearrange("p (a n) -> p a n", a=2)[:, :, 0:N],
                                                                     in_=St.rearrange("p (a n) -> p a n", a=2)[:, :, 0:N], func=AF.Exp, scale=0.125), reads=Sb, writes=PT.b)
                                if si + 1 < len(steps):
                                    pend = emit_qk(steps[si + 1][0], steps[si + 1][2])
                                tgt = len(fq) if si == len(steps) - 1 else (len(fq) * cumw[si]) // cumw[-1]
                                while fqi < tgt:
                                    fn, args = fq[fqi]
                                    fn(*args)
                                    fqi += 1
                                first, last = idx == 0, idx == len(kts) - 1
                                rd = PT.b + [VP.b[kti]]
                                k.op("pe", lambda: nc.tensor.matmul(O.t[:, 0:N], lhsT=VP.t[:, kt, 0:128], rhs=PT.t[:, 0:N], start=first, stop=last), reads=rd, writes=[O.b])
                                k.op("pe", lambda: nc.tensor.matmul(Lp.t[:, 0:N], lhsT=VP.t[:, kt, 64:192], rhs=PT.t[:, 512:512 + N], start=first, stop=last), reads=rd, writes=[Lp.b])
                                if last:
                                    Ls = LsR.next()
                                    Os = OsR.next()
                                    k.op("act", lambda: A.activation(out=Os.t[0:64, 0:N], in_=O.t[0:64, 0:N], func=AF.Copy), reads=[O.b], writes=Os.b)
                                    k.op("dve", lambda: V.tensor_copy(out=Ls.t[0:64, 0:N], in_=O.t[64:128, 0:N]), reads=[O.b], writes=Ls.b)
                                    k.op("act", lambda: A.activation(out=Os.t[64:128, 0:N], in_=Lp.t[64:128, 0:N], func=AF.Copy), reads=[Lp.b], writes=Os.b)
                                    k.op("dve", lambda: V.tensor_copy(out=Ls.t[64:128, 0:N], in_=Lp.t[0:64, 0:N]), reads=[Lp.b], writes=Ls.b)
                                    k.op("dve", lambda: V.reciprocal(out=Ls.t[:, 0:N], in_=Ls.t[:, 0:N]), reads=Ls.b, writes=Ls.b)
                                    k.op("pool", lambda: Pq.tensor_tensor(out=QT.t[:, c, T0:T0 + N], in0=Os.t[:, 0:N], in1=Ls.t[:, 0:N], op=ALU.mult),
                                         reads=Ls.b + Os.b, writes=[QTb[c][ti]])
                            if not isctx:
                                assert fqi == len(fq)
                                for fc in range(2):
                                    k.op("dve", lambda: V.tensor_copy(out=MXF.t[:, fc, 0:N], in_=yp[fc].t[:, 0:N]), reads=[yp[fc].b], writes=[MXF.b[fc]])
                            seq0 = 0 if isctx else 2
                            nsub = 2 if isctx else 32
                            for pc in range(2):
                                dps = PS4.next()
                                for j in range(N // 128):
                                    sj = st0 - seq0 + j
                                    lst = []
                                    if sj > 0:
                                        lst.append((sj - 1, 0))
                                    lst.append((sj, 2 if sj == 0 else (4 if sj == nsub - 1 else 3)))
                                    if sj < nsub - 1:
                                        lst.append((sj + 1, 1))
                                    for idx, (sk, var) in enumerate(lst):
                                        skt = seq0 + sk
                                        skti = 0 if skt < 2 else 1 + (skt - 2) // 4
                                        for gg in range(2):
                                            g = 2 * pc + gg
                                            bo = (g * 5 + var) * 128
                                            k.op("pe", lambda: nc.tensor.matmul(dps.t[gg * 64:(gg + 1) * 64, j * 128:(j + 1) * 128], lhsT=VP.t[:, skt, 192 + g * 64:192 + (g + 1) * 64],
                                                                                rhs=bands.t[:, bo:bo + 128], start=(idx == 0), stop=(idx == len(lst) - 1)),
                                                 reads=[VP.b[skti]], writes=[dps.b])
                                dT = dTR.next()
                                k.op("dve", lambda: V.tensor_copy(out=dT.t[:, 0:N], in_=dps.t[:, 0:N]), reads=[dps.b], writes=dT.b)
                                yps = PS4.next()
                                k.op("pe", lambda: nc.tensor.matmul(yps.t[:, 0:N], lhsT=wpb.t[:, l, pc, :], rhs=dT.t[:, 0:N], start=True, stop=True), reads=dT.b, writes=[yps.b])
                                k.op("dve", lambda: V.tensor_scalar(out=MXP.t[:, pc, 0:N], in0=yps.t[:, 0:N], scalar1=pscale.t[:, l, pc:pc + 1], scalar2=None, op0=ALU.mult),
                                     reads=[yps.b], writes=[MXP.b[pc]])
                            if isctx:
                                for fc in range(2):
                                    yps = PS4.next()
                                    n = 0
                                    for lt in range(2):
                                        for csi in range(2):
                                            k.op("pe", lambda: nc.tensor.matmul(yps.t[:, 0:N], lhsT=UAB.t[:, lt, (csi * 2 + fc) * 128:(csi * 2 + fc + 1) * 128], rhs=d256.t[:, csi, lt, :],
                                                                                start=(n == 0), stop=(n == 3)), reads=[UAB.b[0]], writes=[yps.b])
                                            n += 1
                                    k.op("dve", lambda: V.tensor_copy(out=MXF.t[:, fc, 0:N], in_=yps.t[:, 0:N]), reads=[yps.b], writes=[MXF.b[fc]])
                            if debug and l == 0 and b == 0:
                                k.dma("sp", dbg["mx"][:, 0:4, T0:T0 + N], QT.t[:, :, T0:T0 + N], reads=[QTb[c][ti] for c in range(4)])
                                k.dma("sp", dbg["mx"][:, 4:6, T0:T0 + N], MXP.t[:, :, 0:N], reads=MXP.b)
                                k.dma("sp", dbg["mx"][:, 6:8, T0:T0 + N], MXF.t[:, :, 0:N], reads=MXF.b)
                            ssp = PS4.next()

                            def ss_mm(oc):
                                k.op("pe", lambda: nc.tensor.matmul(ssp.t[:, 0:N], lhsT=ones.t[:], rhs=sq.t[:, oc, 0:N], start=(oc == 0), stop=(oc == 7)),
                                     reads=[sq.b[oc]], writes=[ssp.b])
                            psr = Rot([p for p in PS4.items if p is not ssp])
                            for oc in range(8):
                                ps = psr.next()
                                for mc in range(8):
                                    if mc < 4:
                                        rhs, rb = QT.t[:, mc, T0:T0 + N], [QTb[mc][ti]]
                                    elif mc < 6:
                                        rhs, rb = MXP.t[:, mc - 4, 0:N], [MXP.b[mc - 4]]
                                    else:
                                        rhs, rb = MXF.t[:, mc - 6, 0:N], [MXF.b[mc - 6]]
                                    k.op("pe", lambda: nc.tensor.matmul(ps.t[:, 0:N], lhsT=wout.t[:, mc, oc * 128:(oc + 1) * 128], rhs=rhs, start=(mc == 0), stop=(mc == 7)),
                                         reads=rb + wout.b, writes=[ps.b])
                                k.op("dve", lambda: V.scalar_tensor_tensor(out=XT.t[:, oc, 0:N], in0=ps.t[:, 0:N], scalar=modT.t[:, l, 16 + oc, col:col + 1], in1=XT.t[:, oc, 0:N],
                                                                           op0=ALU.mult, op1=ALU.add), reads=[ps.b, XT.b[oc]], writes=[XT.b[oc]])
                                k.op("pool", lambda: Pq.tensor_tensor(out=sq.t[:, oc, 0:N], in0=XT.t[:, oc, 0:N], in1=XT.t[:, oc, 0:N], op=ALU.mult), reads=[XT.b[oc]], writes=[sq.b[oc]])
                                if oc > 0:
                                    ss_mm(oc - 1)
                            ss_mm(7)
                            k.op("act", lambda: A.activation(out=lnv.t[:, 0:N], in_=ssp.t[:, 0:N], func=AF.Ln, scale=1.0 / D, bias=EPS), reads=[ssp.b], writes=lnv.b)
                            k.op("act", lambda: A.activation(out=rstd.t[:, 0:N], in_=lnv.t[:, 0:N], func=AF.Exp, scale=-0.5), reads=lnv.b, writes=rstd.b)
                            k.dma("pool", x1s[b].rearrange("c p n -> p c n")[:, :, T0:T0 + N], XT.t[:, :, 0:N], reads=XT.b, writes=[x1b[b][ti]])
                            k.dma("pool", rs2[b][:, T0:T0 + N], rstd.t[:, 0:N], reads=rstd.b, writes=[rsb[b][ti]])
                        k.barrier()
                    if stop_after == "BC":
                        return nc
            with ExitStack() as ph:
                wgu = sb("wgu", [128, 8, 2 * DFF], BF16, n=12, stack=ph)
                wdn = sb("wdn", [128, 22, 1024], BF16, n=6, stack=ph)
                gsrc = d["w_gu"][l].rearrange("c p n -> p c n")
                dsrc = d["w_dn"][l].rearrange("c p n -> p c n")
                for g in range(6):
                    c0, c1 = g * 512, min(DFF, (g + 1) * 512)
                    k.dma("pool", wgu.t[:, :, c0:c1], gsrc[:, :, c0:c1], writes=[wgu.b[2 * g]])
                    k.dma("pool", wgu.t[:, :, DFF + c0:DFF + c1], gsrc[:, :, DFF + c0:DFF + c1], writes=[wgu.b[2 * g + 1]])
                for g in range(6):
                    r0, r1 = g * 4, min(22, (g + 1) * 4)
                    k.dma("pool", wdn.t[:, r0:r1, :], dsrc[:, r0:r1, :], writes=[wdn.b[g]])
                XR = Rot([sb(f"XF{i}", [128, 8, FT], F32, n=8, stack=ph) for i in range(2)])
                RR = Rot([sb(f"RF{i}", [128, FT], F32, stack=ph) for i in range(2)])
                fT = sb("fT", [128, 8, FT], BF16, n=8, stack=ph)
                aT = sb("aT", [128, 22, FT], BF16, n=22, stack=ph)
                xnR = Rot([sb(f"xn{i}", [128, FT], F32, stack=ph) for i in range(2)])
                sgR = Rot([sb(f"sg{i}", [128, FT], F32, stack=ph) for i in range(2)])
                items = [(b, T0, N, isctx) for b in range(NB) for (T0, N, isctx) in FTILES if not (isctx and l == 1)]
                XRs = {}

                def f_load(i):
                    b, T0, N, isctx = items[i]
                    pti = 0 if isctx else 1 + (T0 - CTX) // 512
                    X = XR.next()
                    R = RR.next()
                    XRs[i] = (X, R)
                    k.dma("sp", X.t[:, :, 0:N], x1s[b].rearrange("c p n -> p c n")[:, :, T0:T0 + N], reads=[x1b[b][pti]], writes=X.b)
                    k.dma("sp", R.t[:, 0:N], rs2[b][:, T0:T0 + N], reads=[rsb[b][pti]], writes=R.b)

                def f_norm(i):
                    b, T0, N, isctx = items[i]
                    col = 2 if isctx else b
                    X, R = XRs[i]
                    for c in range(8):
                        xn = xnR.next()
                        k.op("pool", lambda: Pq.tensor_tensor(out=xn.t[:, 0:N], in0=X.t[:, c, 0:N], in1=R.t[:, 0:N], op=ALU.mult), reads=[X.b[c]] + R.b, writes=xn.b)
                        k.op("dve", lambda: V.tensor_scalar(out=fT.t[:, c, 0:N], in0=xn.t[:, 0:N], scalar1=gsc2.t[:, l, c, col:col + 1],
                                                            scalar2=modT.t[:, l, 24 + c, col:col + 1], op0=ALU.mult, op1=ALU.add), reads=xn.b, writes=[fT.b[c]])

                def f_gateup(i):
                    b, T0, N, isctx = items[i]
                    for j in range(22):
                        gps = PSUM.next()
                        ups = PSUM.next()
                        for dc in range(8):
                            k.op("pe", lambda: nc.tensor.matmul(gps.t[:, 0:N], lhsT=wgu.t[:, dc, j * 128:(j + 1) * 128], rhs=fT.t[:, dc, 0:N], start=(dc == 0), stop=(dc == 7)),
                                 reads=[fT.b[dc], wgu.b[2 * (j // 4)]], writes=[gps.b])
                        for dc in range(8):
                            k.op("pe", lambda: nc.tensor.matmul(ups.t[:, 0:N], lhsT=wgu.t[:, dc, DFF + j * 128:DFF + (j + 1) * 128], rhs=fT.t[:, dc, 0:N], start=(dc == 0), stop=(dc == 7)),
                                 reads=[fT.b[dc], wgu.b[2 * (j // 4) + 1]], writes=[ups.b])
                        sg = sgR.next()
                        k.op("act", lambda: A.activation(out=sg.t[:, 0:N], in_=gps.t[:, 0:N], func=AF.Silu), reads=[gps.b], writes=sg.b)
                        k.op("dve", lambda: V.tensor_tensor(out=aT.t[:, j, 0:N], in0=ups.t[:, 0:N], in1=sg.t[:, 0:N], op=ALU.mult), reads=[ups.b] + sg.b, writes=[aT.b[j]])

                def f_down(i):
                    b, T0, N, isctx = items[i]
                    col = 2 if isctx else b
                    pti = 0 if isctx else 1 + (T0 - CTX) // 512
                    X, R = XRs[i]
                    for oc in range(8):
                        dps = PSUM.next()
                        for j in range(22):
                            k.op("pe", lambda: nc.tensor.matmul(dps.t[:, 0:N], lhsT=wdn.t[:, j, oc * 128:(oc + 1) * 128], rhs=aT.t[:, j, 0:N], start=(j == 0), stop=(j == 21)),
                                 reads=[aT.b[j], wdn.b[j // 4]], writes=[dps.b])
                        k.op("dve", lambda: V.scalar_tensor_tensor(out=X.t[:, oc, 0:N], in0=dps.t[:, 0:N], scalar=modT.t[:, l, 40 + oc, col:col + 1], in1=X.t[:, oc, 0:N],
                                                                   op0=ALU.mult, op1=ALU.add), reads=[dps.b, X.b[oc]], writes=[X.b[oc]])
                    if l == 0:
                        k.dma("pool", x2s[b].rearrange("c p n -> p c n")[:, :, T0:T0 + N], X.t[:, :, 0:N], reads=X.b, writes=[x2b[b][pti]])
                    else:
                        k.dma("pool", yT[b].rearrange("c p n -> p c n")[:, :, T0 - CTX:T0 - CTX + N], X.t[:, :, 0:N], reads=X.b)

                f_load(0)
                f_norm(0)
                for i in range(len(items)):
                    if i + 1 < len(items):
                        f_load(i + 1)
                    f_gateup(i)
                    if i + 1 < len(items):
                        f_norm(i + 1)
                    f_down(i)
                k.barrier()
            if stop_after == "F":
                return nc
        k.barrier()
    return nc


_CONST = {}


def _constants():
    if _CONST:
        return _CONST
    n_freq = 16
    freqs = (10000.0 ** (-np.arange(n_freq, dtype=np.float32) / n_freq)).astype(np.float32)
    t = np.arange(T)
    pos = [(t // 64).astype(np.float32), (t % 64).astype(np.float32)]
    cos = np.zeros((128, T), np.float32)
    sin = np.zeros((128, T), np.float32)
    prot = np.zeros((128, 128), np.float32)
    for p in range(128):
        dd = p % 64
        s, half, f = dd // 32, (dd % 32) // 16, dd % 16
        ang = (pos[s] * freqs[f]).astype(np.float32)
        cos[p] = np.cos(ang)
        sin[p] = np.sin(ang) * (-1.0 if half == 0 else 1.0)
        partner = p + 16 if half == 0 else p - 16
        prot[partner, p] = 1.0
    _CONST["rope_cos"] = cos
    _CONST["rope_sin"] = sin
    _CONST["prot"] = prot.astype(NPBF)
    c64 = np.arange(64)
    ang = 2 * np.pi * np.outer(c64, c64) / 64.0
    cc, sc = np.cos(ang), np.sin(ang)
    z = np.zeros((64, 64))
    _CONST["ccbd"] = np.block([[cc, z], [z, cc]]).astype(np.float32)
    _CONST["scbd"] = np.block([[sc, z], [z, sc]]).astype(np.float32)
    li = np.arange(T, dtype=np.int64)
    m = (np.outer(li, li) % T).astype(np.float64) * (2 * np.pi / T)
    _CONST["dft_c"] = (np.cos(m) / 512.0).astype(np.float32).astype(NPBF).reshape(32, 128, T)
    _CONST["dft_s"] = (-np.sin(m) / 512.0).astype(np.float32).astype(NPBF).reshape(32, 128, T)
    l2 = np.arange(CTX, dtype=np.int64)
    m2 = (np.outer(l2, l2) % CTX).astype(np.float64) * (2 * np.pi / CTX)
    c2 = (np.cos(m2) / 128.0).reshape(2, 128, CTX)
    s2 = (-np.sin(m2) / 128.0).reshape(2, 128, CTX)
    _CONST["dft256"] = np.stack([c2, s2], 0).transpose(2, 0, 1, 3).astype(np.float32).astype(NPBF)
    Ls = 384
    bands = np.zeros((128, 4, 5, 128), np.float32)
    tt = np.arange(Ls)
    for g, w in enumerate(POOLW):
        lo = np.clip(tt - w // 2, 0, Ls)
        hi = np.clip(tt + w - w // 2, 0, Ls)
        Dm = np.zeros((Ls, Ls), np.float64)
        for tp in range(Ls):
            Dm[lo[tp]:hi[tp], tp] = 1.0 / (hi[tp] - lo[tp])
            Dm[tp, tp] -= 1.0
        bands[:, g, 0] = Dm[0:128, 128:256]
        bands[:, g, 1] = Dm[256:384, 128:256]
        bands[:, g, 2] = Dm[0:128, 0:128]
        bands[:, g, 3] = Dm[128:256, 128:256]
        bands[:, g, 4] = Dm[256:384, 256:384]
    _CONST["bands"] = bands.reshape(128, 2560).astype(NPBF)
    return _CONST


def _bd(a, b):
    z = np.zeros((64, 64), np.float32)
    return np.block([[a, z], [z, b]])


def _prep_shared(inp):
    f = lambda a: np.ascontiguousarray(np.asarray(a, dtype=np.float32))
    sh = dict(_constants())
    sh["w_ada"] = f(inp["w_ada"]).reshape(2, 8, 128, 6144)
    sh["b_ada"] = f(f(inp["b_ada"]).reshape(2, 48, 128).transpose(0, 2, 1))
    sh["g_mix"] = f(f(inp["g_mix"]).reshape(2, 8, 128).transpose(0, 2, 1))
    sh["g_ffn"] = f(f(inp["g_ffn"]).reshape(2, 8, 128).transpose(0, 2, 1))
    w_in = f(inp["w_in"])
    qcols = []
    for c in range(4):
        qcols += list(range(64 * c, 64 * c + 64)) + list(range(64 * (c + 4), 64 * (c + 4) + 64))
    cols = qcols + list(range(512, 640)) + list(range(1024, 1280)) + list(range(640, 768)) + list(range(768, 1024))
    sh["w_in"] = f(w_in[:, :, cols]).reshape(2, 8, 128, 1280)
    qg, kg = f(inp["q_gain"]), f(inp["k_gain"])
    sh["qk_gain"] = f(np.stack([np.tile(qg, (1, 2)), np.tile(kg, (1, 2))], axis=-1))
    wp, wf = f(inp["w_pool"]), f(inp["w_four"])
    sh["w_pool"] = f(np.stack([np.stack([_bd(wp[l, 2 * pc], wp[l, 2 * pc + 1]) for pc in range(2)], 1) for l in range(2)], 0))
    sh["w_four"] = f(np.stack([np.stack([_bd(wf[l, 2 * pc], wf[l, 2 * pc + 1]) for pc in range(2)], 1) for l in range(2)], 0))
    sh["pool_scale"] = f(f(inp["pool_scale"]).reshape(2, 2, 128).transpose(0, 2, 1))
    w_out = f(inp["w_out"])
    rows = qcols + list(range(512, 1024))
    sh["w_out"] = f(w_out[:, rows, :]).reshape(2, 8, 128, 1024)
    sh["w_gu"] = f(inp["w_gate_up"]).reshape(2, 8, 128, 2 * DFF)
    sh["w_dn"] = f(inp["w_down"]).reshape(2, 22, 128, 1024)
    return sh


def _prep_core(inp, i):
    f = lambda a: np.ascontiguousarray(np.asarray(a, dtype=np.float32))
    b0 = NB * i
    x = np.asarray(inp["x"])[b0:b0 + NB]
    ctx = np.asarray(inp["ctx"])[b0:b0 + NB]
    c = np.asarray(inp["c"])[b0:b0 + NB]
    cctx = np.asarray(inp["c_ctx"])
    m = {}
    m["xT"] = f(x.transpose(0, 2, 1)).reshape(NB, 8, 128, T)
    m["ctxT"] = f(ctx.transpose(0, 2, 1)).reshape(NB, 8, 128, CTX)
    cc = np.stack([c[0], c[1], cctx, cctx], axis=1)
    m["cc"] = f(cc.reshape(8, 128, 4).transpose(1, 0, 2))
    return m


_NC_CACHE = {}


def kernel(**inputs):
    if "nc" not in _NC_CACHE:
        _NC_CACHE["nc"] = build_program()
    nc = _NC_CACHE["nc"]
    sh = _prep_shared(inputs)
    in_maps = []
    for i in range(NCORES):
        m = dict(sh)
        m.update(_prep_core(inputs, i))
        in_maps.append(m)
    res = run_bass_kernel_spmd(nc, in_maps, core_ids=list(range(NCORES)))
    outs = []
    for i in range(NCORES):
        y = np.asarray(res.results[i]["yT"]).reshape(NB, D, T)
        outs.append(y.transpose(0, 2, 1))
    return np.ascontiguousarray(np.concatenate(outs, axis=0).astype(np.float32))
```
